# Optimizing a Trainium2 kernel written in Bass

```python
import math
import jax, jax.numpy as jnp
from jax import lax
import numpy as np

D_MODEL = 1024
BATCH = 8
SEQ = 4096
DEPTH = 1

HG_HEADS = 4
HG_KDIM = 128
HG_VDIM = 128
HG_WIDTH = HG_HEADS * HG_KDIM
CHUNK = 32
LRU_WIDTH = 512
LRU_BLOCKS = 8
LRU_BLOCK_DIM = LRU_WIDTH // LRU_BLOCKS
CONV_WIDTH = 4
LRU_C = 8.0
D_MIX = HG_WIDTH + LRU_WIDTH
D_IN = 4 * HG_WIDTH + 2 * LRU_WIDTH
D_FF = -(-8 * D_MODEL // (3 * 256)) * 256
EPS = 1e-6

kernel_name = 'hybrid_hgrn2_rglru_parallel_heads'


def rms_norm(x, w):
    xf = x.astype(jnp.float32)
    y = xf * lax.rsqrt(jnp.mean(xf * xf, axis=-1, keepdims=True) + EPS)
    return (y * w.astype(jnp.float32)).astype(x.dtype)


def hgrn2_mix(q, f_logit, i, g, lb, norm_w):
    B, S, _ = q.shape
    out_dtype = q.dtype
    qf = jax.nn.silu(q.astype(jnp.float32))
    f = lb + (1.0 - lb) * jax.nn.sigmoid(f_logit.astype(jnp.float32))
    log_f = jnp.log(f)
    k = 1.0 - f
    v = i.astype(jnp.float32)
    n_chunks = S // CHUNK

    def to_chunks(t, d):
        return t.reshape(B, n_chunks, CHUNK, HG_HEADS, d).transpose(1, 0, 3, 2, 4)

    qc, kc, gc, vc = to_chunks(qf, HG_KDIM), to_chunks(k, HG_KDIM), to_chunks(log_f, HG_KDIM), to_chunks(v, HG_VDIM)
    causal = jnp.tril(jnp.ones((CHUNK, CHUNK), dtype=bool))

    def step(state, xs):
        q_, k_, lg, v_ = xs
        b = jnp.cumsum(lg, axis=-2)
        b_last = b[..., -1:, :]
        q_dec = q_ * jnp.exp(b)
        k_dec = k_ * jnp.exp(-b)
        scores = jnp.where(causal, jnp.einsum('bhtk,bhsk->bhts', q_dec, k_dec), 0.0)
        o = jnp.einsum('bhts,bhsv->bhtv', scores, v_) + jnp.einsum('bhtk,bhkv->bhtv', q_dec, state)
        k_state = k_ * jnp.exp(b_last - b)
        new_state = jnp.exp(b_last)[..., 0, :, None] * state + jnp.einsum('bhsk,bhsv->bhkv', k_state, v_)
        return new_state, o

    state0 = jnp.zeros((B, HG_HEADS, HG_KDIM, HG_VDIM), jnp.float32)
    _, o = lax.scan(step, state0, (qc, kc, gc, vc))
    o = o.transpose(1, 0, 3, 2, 4).reshape(B, S, HG_HEADS, HG_VDIM)
    o = o * lax.rsqrt(jnp.mean(o * o, axis=-1, keepdims=True) + EPS) * norm_w.astype(jnp.float32)
    o = o * jax.nn.silu(g.astype(jnp.float32).reshape(B, S, HG_HEADS, HG_VDIM))
    return o.reshape(B, S, HG_WIDTH).astype(out_dtype)


def rglru_mix(xb, gate_b, conv_w, conv_b, wa, ba, wx, bx, a_param):
    B, S, W = xb.shape
    out_dtype = xb.dtype
    x_pad = jnp.pad(xb, ((0, 0), (CONV_WIDTH - 1, 0), (0, 0)))
    xc = conv_b + sum(x_pad[:, tap:tap + S] * conv_w[tap] for tap in range(CONV_WIDTH))
    xc = xc.astype(jnp.float32)
    xblk = xc.reshape(B, S, LRU_BLOCKS, LRU_BLOCK_DIM)
    r = jax.nn.sigmoid(jnp.einsum('bsnd,nde->bsne', xblk, wa.astype(jnp.float32)).reshape(B, S, W) + ba)
    ig = jax.nn.sigmoid(jnp.einsum('bsnd,nde->bsne', xblk, wx.astype(jnp.float32)).reshape(B, S, W) + bx)
    log_a = -LRU_C * r * jax.nn.softplus(-a_param.astype(jnp.float32))
    a = jnp.exp(log_a)
    b_in = jnp.sqrt(-jnp.expm1(2.0 * log_a)) * (ig * xc)

    def combine(lhs, rhs):
        a1, b1 = lhs
        a2, b2 = rhs
        return a1 * a2, a2 * b1 + b2

    _, h = lax.associative_scan(combine, (a, b_in), axis=1)
    y = h * jax.nn.gelu(gate_b.astype(jnp.float32), approximate=True)
    return y.astype(out_dtype)


def setup_inputs(seed: int = 0) -> dict:
    key = jax.random.key(seed)
    ks = jax.random.split(key, 20)
    f32 = jnp.float32
    nrm = lambda k, shape, scale: (jax.random.normal(k, shape, f32) * scale)
    s_lru = jax.random.uniform(ks[12], (DEPTH, LRU_WIDTH), f32, 0.9, 0.999) ** (1.0 / LRU_C)
    return {
        'x': nrm(ks[0], (BATCH, SEQ, D_MODEL), 1.0),
        'mix_norm_w': 1.0 + nrm(ks[1], (DEPTH, D_MODEL), 0.02),
        'w_in': nrm(ks[2], (DEPTH, D_MODEL, D_IN), D_MODEL ** -0.5),
        'hg_lb': nrm(ks[3], (DEPTH + 1, HG_WIDTH), 0.5),
        'hg_norm_w': 1.0 + nrm(ks[4], (DEPTH, HG_VDIM), 0.02),
        'conv_w': nrm(ks[5], (DEPTH, CONV_WIDTH, LRU_WIDTH), CONV_WIDTH ** -0.5),
        'conv_b': nrm(ks[6], (DEPTH, LRU_WIDTH), 0.02),
        'lru_wa': nrm(ks[7], (DEPTH, LRU_BLOCKS, LRU_BLOCK_DIM, LRU_BLOCK_DIM), LRU_BLOCK_DIM ** -0.5),
        'lru_ba': nrm(ks[8], (DEPTH, LRU_WIDTH), 0.02),
        'lru_wx': nrm(ks[9], (DEPTH, LRU_BLOCKS, LRU_BLOCK_DIM, LRU_BLOCK_DIM), LRU_BLOCK_DIM ** -0.5),
        'lru_bx': nrm(ks[10], (DEPTH, LRU_WIDTH), 0.02),
        'lru_a': jnp.log(s_lru) - jnp.log1p(-s_lru),
        'w_out': nrm(ks[11], (DEPTH, D_MIX, D_MODEL), D_MIX ** -0.5),
        'ffn_norm_w': 1.0 + nrm(ks[13], (DEPTH, D_MODEL), 0.02),
        'w_gate_up': nrm(ks[14], (DEPTH, D_MODEL, 2 * D_FF), D_MODEL ** -0.5),
        'w_down': nrm(ks[15], (DEPTH, D_FF, D_MODEL), D_FF ** -0.5),
        'final_norm_w': 1.0 + nrm(ks[16], (D_MODEL,), 0.02),
    }


def reference(x, mix_norm_w, w_in, hg_lb, hg_norm_w, conv_w, conv_b, lru_wa, lru_ba,
              lru_wx, lru_bx, lru_a, w_out, ffn_norm_w, w_gate_up, w_down, final_norm_w):
    lb_all = jnp.cumsum(jax.nn.softmax(hg_lb.astype(jnp.float32), axis=0), axis=0)
    h = x
    for l in range(DEPTH):
        xn = rms_norm(h, mix_norm_w[l])
        proj = jnp.einsum('bsd,de->bse', xn, w_in[l])
        q, f_logit, i_v, g, lru_x, lru_gate = jnp.split(
            proj, [HG_WIDTH, 2 * HG_WIDTH, 3 * HG_WIDTH, 4 * HG_WIDTH, 4 * HG_WIDTH + LRU_WIDTH], axis=-1)
        o_hg = hgrn2_mix(q, f_logit, i_v, g, lb_all[l], hg_norm_w[l])
        o_lru = rglru_mix(lru_x, lru_gate, conv_w[l], conv_b[l], lru_wa[l], lru_ba[l],
                          lru_wx[l], lru_bx[l], lru_a[l])
        mixed = jnp.concatenate([o_hg, o_lru], axis=-1)
        h = h + jnp.einsum('bse,ed->bsd', mixed, w_out[l])
        hn = rms_norm(h, ffn_norm_w[l])
        gate, up = jnp.split(jnp.einsum('bsd,df->bsf', hn, w_gate_up[l]), 2, axis=-1)
        h = h + jnp.einsum('bsf,fd->bsd', jax.nn.silu(gate) * up, w_down[l])
    return rms_norm(h, final_norm_w)
```

```python
import math
import numpy as np
from contextlib import ExitStack
import concourse.bass as bass
import concourse.mybir as mybir
from concourse.bass_utils import run_bass_kernel_spmd

F32 = mybir.dt.float32
BF16 = mybir.dt.bfloat16
AF = mybir.ActivationFunctionType
ALU = mybir.AluOpType

ENGS = ("pe", "act", "dve", "pool", "sp")

D = 1024
SEQ = 4096
TB = 512
NB = SEQ // TB
NT = TB // 128
KC = D // 128
DIN = 3072
DFF = 2816
NF = DFF // 128
NG = NF // 2
EPS = 1e-6
NCOL = 57
C_NW1, C_NW2, C_A0, C_A1, C_HGNW, C_CW, C_CB, C_BA, C_BX, C_LA = 0, 8, 16, 20, 24, 25, 41, 45, 49, 53
V_A, V_B, V_NB, V_HBA, V_HBX, V_C, V_HC, V_EPS, V_ONE, V_LN05, V_TMP = 0, 4, 8, 12, 16, 20, 24, 28, 29, 30, 32
NDV = 48


class Sched:
    def __init__(self, nc):
        self.nc = nc
        self.streams = {e: [] for e in ENGS}
        self.cnt = {e: 0 for e in ENGS}
        self.lw = {}
        self.rd = {}
        self.dma_cnt = {}
        self.waited = {e: {} for e in ENGS}
        self.cut = False

    def stage(self, k):
        import os
        if k > int(os.environ.get("KSTAGE", "99")):
            self.cut = True

    def _deps(self, eng, reads, writes):
        deps = {}

        def add(tok, raw):
            if tok is None:
                return
            sk, v = tok
            if sk == eng and (eng == "pe" or not raw):
                return
            if deps.get(sk, 0) < v:
                deps[sk] = v

        for r in reads:
            add(self.lw.get(r), True)
        for w in writes:
            add(self.lw.get(w), False)
            for t in self.rd.get(w, ()):
                add(t, False)
        out = []
        wd = self.waited[eng]
        for sk, v in deps.items():
            if wd.get(sk, 0) < v:
                wd[sk] = v
                out.append((sk, v))
        return out

    def _commit(self, tok, reads, writes):
        for r in reads:
            self.rd.setdefault(r, []).append(tok)
        for w in writes:
            self.lw[w] = tok
            self.rd[w] = []

    @staticmethod
    def _is_ps(k):
        n = k[0] if isinstance(k, tuple) else k
        return n in ("pG", "pTR", "pSC", "pPS")

    def op(self, eng, fn, reads=(), writes=()):
        if self.cut:
            return
        writes = tuple(writes) + tuple(k for k in reads if self._is_ps(k))
        reads = tuple(k for k in reads if not self._is_ps(k))
        waits = self._deps(eng, reads, writes)
        self.cnt[eng] += 1
        self._commit((eng, self.cnt[eng]), reads, writes)
        self.streams[eng].append((waits, fn, None, 0))

    def dma(self, eng, fn, sem, n=1, reads=(), writes=()):
        if self.cut:
            return
        reads = tuple(reads)
        writes = tuple(writes)
        waits = self._deps(eng, reads, writes)
        sk = ("dma", sem)
        self.dma_cnt[sk] = self.dma_cnt.get(sk, 0) + n
        self._commit((sk, 16 * self.dma_cnt[sk]), reads, writes)
        self.streams[eng].append((waits, fn, sk, n))

    def final_wait(self, eng, keys):
        waits = self._deps(eng, (), tuple(keys))
        self.streams[eng].append((waits, None, None, 0))

    def emit(self):
        nc = self.nc
        with ExitStack() as st:
            sems = {}
            for e in ENGS:
                sems[e] = st.enter_context(nc.semaphore("s_" + e))
            for i, sk in enumerate(self.dma_cnt):
                sems[sk] = st.enter_context(nc.semaphore("d%d" % i))
            block = st.enter_context(nc.Block())

            def run(e_name, e):
                for waits, fn, dsk, n in self.streams[e_name]:
                    for sk, v in waits:
                        e.wait_ge(sems[sk], v)
                    if fn is None:
                        continue
                    r = fn(e)
                    if dsk is None:
                        if isinstance(r, (list, tuple)):
                            r = r[-1]
                        r.then_inc(sems[e_name], 1)
                    else:
                        if not isinstance(r, (list, tuple)):
                            r = [r]
                        assert len(r) == n, (len(r), n)
                        for ins in r:
                            ins.then_inc(sems[dsk], 16)

            @block.tensor
            def _(e):
                run("pe", e)

            @block.scalar
            def _(e):
                run("act", e)

            @block.vector
            def _(e):
                run("dve", e)

            @block.gpsimd
            def _(e):
                run("pool", e)

            @block.sync
            def _(e):
                run("sp", e)


def build_program(nblocks=NB):
    nc = bass.Bass("TRN2", target_bir_lowering=False)

    def din(name, shape, dt=F32):
        return nc.dram_tensor(name, shape, dt, kind="ExternalInput").ap()

    x_d = din("x", [SEQ, D])
    win_d = din("w_in", [D, DIN])
    wout_d = din("w_out", [D, D])
    wgu_d = din("w_gu", [D, 2 * DFF])
    wdn_d = din("w_down", [DFF, D])
    pvec_d = din("pvec", [128, NCOL])
    wabd_d = din("wabd", [128, 4, 128])
    wxbd_d = din("wxbd", [128, 4, 128])
    finw_d = din("finw", [1, D])
    ident_d = din("ident", [128, 128])
    trim_d = din("trimask", [128, 128])
    scanm_d = din("scanmask", [1, TB])
    ones_d = din("onesm", [128, 128])
    out_d = nc.dram_tensor("out", [SEQ, D], F32, kind="ExternalOutput").ap()
    sgu_d = nc.dram_tensor("sgu", [NG, 128, 4096], BF16, kind="Internal").ap()
    sout_d = nc.dram_tensor("sout", [128, 8192], BF16, kind="Internal").ap()

    S = Sched(nc)
    with ExitStack() as st:
        def sb(name, shape, dt=F32):
            return st.enter_context(nc.sbuf_tensor(name, shape, dt))

        def ps(name, shape, dt=F32):
            return st.enter_context(nc.psum_tensor(name, shape, dt))

        win = sb("win", [128, KC, DIN], BF16)
        wdn = sb("wdn", [128, NF, D], BF16)
        ring = sb("ring", [128, 8192], BF16)
        hb = sb("hb", [128, 5, D], F32)
        actA = sb("actA", [128, KC, TB], BF16)
        actB = sb("actB", [128, KC, TB], BF16)
        U = sb("U", [128, NF * TB // 2], F32)
        Ub = U.bitcast(BF16)
        sgt = sb("sgt", [128, 2, TB], F32)
        kdlo = sb("kdlo", [128, 2, NT, 128], BF16)
        kdhi = sb("kdhi", [128, 2, NT, 128], BF16)
        QD = sb("QD", [128, 2, TB], BF16)
        KD = sb("KD", [128, 2, TB], BF16)
        scm = sb("scm", [128, 2, NT, 128], BF16)
        sgs = sb("sgs", [128, 2, TB], BF16)
        Spb = sb("Spb", [128, 2, 128], BF16)
        Sm = sb("Sm", [128, 4, 128], F32)
        dcol = sb("dcol", [128, 2, 8], F32)
        xbuf = sb("xbuf", [128, 2, TB + 3], F32)
        halo = sb("halo", [128, 4, 3], F32)
        hcar = sb("hcar", [128, 4], F32)
        xcb = sb("xcb", [128, 2, TB], BF16)
        identb = sb("identb", [128, 128], BF16)
        trimb = sb("trimb", [128, 128], BF16)
        scanmb = sb("scanmb", [128, TB], BF16)
        onesf = sb("onesf", [128, 128], F32)
        finw = sb("finw_s", [128, D], F32)
        wab = sb("wab", [128, 4, 128], BF16)
        wxb = sb("wxb", [128, 4, 128], BF16)
        pv = sb("pv", [128, NCOL], F32)
        dv = sb("dv", [128, NDV], F32)
        ssq = sb("ssq", [128, 16], F32)
        rstd = sb("rstd", [128, 16], F32)

        pG = [ps("pg%d" % i, [128, TB], F32) for i in range(5)]
        pTR = ps("ptr", [128, 1024], BF16)
        pSC = ps("psc", [128, NT, 128], F32)
        pPS = ps("pps", [128, 4, 128], F32)

        def T(i):
            return U[:, i * TB:(i + 1) * TB]

        def TK(i):
            return (("U", 2 * i), ("U", 2 * i + 1))

        def ACT(out, in_, func, reads, writes, scale=1.0, bias=None, accum=None):
            def fn(e):
                kw = dict(out=out, in_=in_, func=func, scale=scale)
                if bias is not None:
                    kw["bias"] = bias
                if accum is not None:
                    kw["accum_out"] = accum
                return e.activation(**kw)
            S.op("act", fn, reads, writes)

        def TS(out, in0, s1, s2, op0, op1, reads, writes, eng="dve"):
            def fn(e):
                if s2 is None:
                    return e.tensor_scalar(out=out, in0=in0, scalar1=s1, scalar2=None, op0=op0)
                return e.tensor_scalar(out=out, in0=in0, scalar1=s1, scalar2=s2, op0=op0, op1=op1)
            S.op(eng, fn, reads, writes)

        def TT(out, in0, in1, op, reads, writes, eng="dve"):
            S.op(eng, lambda e: e.tensor_tensor(out=out, in0=in0, in1=in1, op=op), reads, writes)

        def STT(out, in0, scalar, in1, op0, op1, reads, writes):
            S.op("dve", lambda e: e.scalar_tensor_tensor(out=out, in0=in0, scalar=scalar, in1=in1, op0=op0, op1=op1), reads, writes)

        def SCAN(out, d0, d1, init, reads, writes):
            S.op("dve", lambda e: e.tensor_tensor_scan(out=out, data0=d0, data1=d1, initial=init, op0=ALU.mult, op1=ALU.add), reads, writes)

        def COPY(eng, out, in_, reads, writes):
            if eng == "act":
                ACT(out, in_, AF.Copy, reads, writes)
            else:
                S.op(eng, lambda e: e.tensor_copy(out=out, in_=in_), reads, writes)

        def MEMSET(eng, ap, val, writes):
            S.op(eng, lambda e: e.memset(ap, val), (), writes)

        gstate = {"g": 0}

        def gbank(pool):
            i = pool[gstate["g"] % len(pool)]
            gstate["g"] += 1
            return i

        GM = [0, 1, 2]
        GO = [3, 4]
        GF = [0, 1, 2, 3, 4]

        def col(c):
            return pv[:, c:c + 1]

        def dcolv(c):
            return dv[:, c:c + 1]

        S.dma("sp", lambda e: e.dma_start(out=pv[:], in_=pvec_d), "pv", writes=["pv"])
        S.dma("sp", lambda e: e.dma_start(out=onesf[:], in_=ones_d), "onesf", writes=["onesf"])
        S.dma("sp", lambda e: e.dma_start(out=finw[:], in_=finw_d.partition_broadcast(128)), "finw", writes=["finw"])
        S.dma("pool", lambda e: e.dma_start(out=identb[:], in_=ident_d), "identb", writes=["identb"])
        S.dma("pool", lambda e: e.dma_start(out=trimb[:], in_=trim_d), "trimb", writes=["trimb"])
        S.dma("pool", lambda e: e.dma_start(out=scanmb[:], in_=scanm_d.partition_broadcast(128)), "scanmb", writes=["scanmb"])
        S.dma("pool", lambda e: e.dma_start(out=wab[:], in_=wabd_d), "wab", writes=["wab"])
        S.dma("pool", lambda e: e.dma_start(out=wxb[:], in_=wxbd_d), "wxb", writes=["wxb"])

        def load_win(e):
            return [e.dma_start(out=win[:, kc, :], in_=win_d[kc * 128:(kc + 1) * 128, :]) for kc in range(KC)]
        S.dma("pool", load_win, "win", n=KC, writes=["win"])

        def load_wdn(e):
            return [e.dma_start(out=wdn[:, f, :], in_=wdn_d[f * 128:(f + 1) * 128, :]) for f in range(NF)]

        MEMSET("dve", Sm[:], 0.0, ["Sm"])
        MEMSET("dve", halo[:], 0.0, ["halo"])
        MEMSET("dve", hcar[:], 0.0, ["hcar"])
        MEMSET("dve", kdlo[:], 0.0, [("kdlo", 0), ("kdlo", 1)])
        MEMSET("dve", kdhi[:], 0.0, [("kdhi", 0), ("kdhi", 1)])
        MEMSET("dve", dv[:, V_EPS:V_EPS + 1], EPS, ["dv_c"])
        MEMSET("dve", dv[:, V_ONE:V_ONE + 1], 1.0, ["dv_c"])
        MEMSET("dve", dv[:, V_LN05:V_LN05 + 1], math.log(0.5), ["dv_c"])
        TT(dv[:, V_TMP:V_TMP + 4], pv[:, C_A0:C_A0 + 4], pv[:, C_A1:C_A1 + 4], ALU.subtract, ["pv"], ["dv_t"])
        ACT(dv[:, V_TMP:V_TMP + 4], dv[:, V_TMP:V_TMP + 4], AF.Tanh, ["dv_t"], ["dv_t"], scale=0.5)
        TS(dv[:, V_A:V_A + 4], dv[:, V_TMP:V_TMP + 4], 0.25, 0.75, ALU.mult, ALU.add, ["dv_t"], ["dv_ab"])
        TS(dv[:, V_B:V_B + 4], dv[:, V_TMP:V_TMP + 4], -0.25, 0.25, ALU.mult, ALU.add, ["dv_t"], ["dv_ab"])
        TS(dv[:, V_NB:V_NB + 4], dv[:, V_TMP:V_TMP + 4], 0.25, -0.25, ALU.mult, ALU.add, ["dv_t"], ["dv_ab"])
        TS(dv[:, V_HBA:V_HBA + 4], pv[:, C_BA:C_BA + 4], 0.5, None, ALU.mult, None, ["pv"], ["dv_lru"])
        TS(dv[:, V_HBX:V_HBX + 4], pv[:, C_BX:C_BX + 4], 0.5, None, ALU.mult, None, ["pv"], ["dv_lru"])
        ACT(dv[:, V_TMP + 4:V_TMP + 8], pv[:, C_LA:C_LA + 4], AF.Exp, ["pv"], ["dv_t2"], scale=-1.0)
        ACT(dv[:, V_TMP + 4:V_TMP + 8], dv[:, V_TMP + 4:V_TMP + 8], AF.Ln, ["dv_t2", "dv_c"], ["dv_t2"], bias=dcolv(V_ONE))
        TS(dv[:, V_C:V_C + 4], dv[:, V_TMP + 4:V_TMP + 8], -8.0, None, ALU.mult, None, ["dv_t2"], ["dv_lru"])
        TS(dv[:, V_HC:V_HC + 4], dv[:, V_TMP + 4:V_TMP + 8], -4.0, None, ALU.mult, None, ["dv_t2"], ["dv_lru"])

        ring_slot = lambda s: ring[:, s * 4096:(s + 1) * 4096].rearrange("p (k c) -> p k c", k=KC)
        wout_v = ring[:, :].rearrange("p (e d) -> p e d", e=KC)

        def hslot(b, tt):
            return (4 * b + tt) % 5

        def load_x(b, tts=range(NT)):
            for tt in tts:
                s = hslot(b, tt)
                r0 = b * TB + tt * 128
                S.dma("sp", (lambda s, r0: lambda e: e.dma_start(out=hb[:, s, :], in_=x_d[r0:r0 + 128, :]))(s, r0),
                      ("hb", s), writes=[("hb", s)])

        def rms_stats(b, base, junk_slots):
            for tt in range(NT):
                s = hslot(b, tt)
                jk = junk_slots[tt % 2]
                ACT(Ub[:, jk * 1024:(jk + 1) * 1024], hb[:, s, :], AF.Square, [("hb", s)],
                    [("U", 2 * jk), ("U", 2 * jk + 1), ("ssq", base + tt)], accum=ssq[:, base + tt:base + tt + 1])
            ks = [("ssq", base + tt) for tt in range(NT)]
            kr = [("rstd", base + tt) for tt in range(NT)]
            ACT(rstd[:, base:base + 4], ssq[:, base:base + 4], AF.Ln, ks + ["dv_c"], kr, scale=1.0 / D, bias=dcolv(V_EPS))
            ACT(rstd[:, base:base + 4], rstd[:, base:base + 4], AF.Exp, kr, kr, scale=-0.5)

        def norm_transpose(b, base, dst, dkey, nwc):
            for tt in range(NT):
                s = hslot(b, tt)
                jk = tt % 2
                xnb = Ub[:, jk * 1024:(jk + 1) * 1024]
                kx = [("U", 2 * jk), ("U", 2 * jk + 1)]
                TS(xnb, hb[:, s, :], rstd[:, base + tt:base + tt + 1], None, ALU.mult, None,
                   [("hb", s), ("rstd", base + tt)], kx)

                def tr(e, xnb=xnb):
                    r = None
                    for kc in range(KC):
                        r = e.transpose(pTR[:, kc * 128:(kc + 1) * 128], xnb[:, kc * 128:(kc + 1) * 128], identb[:])
                    return r
                S.op("pe", tr, kx + ["identb"], ["pTR"])
                TT(dst[:, :, tt * 128:(tt + 1) * 128], pTR[:, :].rearrange("p (k c) -> p k c", k=KC),
                   pv[:, nwc:nwc + KC].unsqueeze(2).to_broadcast([128, KC, 128]), ALU.mult,
                   ["pTR", "pv"], [dkey])

        def proj_fm(e0, bank, src, skey):
            def fn(e):
                r = None
                for kc in range(KC):
                    r = e.matmul(pG[bank][:, :], lhsT=win[:, kc, e0:e0 + 128], rhs=src[:, kc, :],
                                 start=(kc == 0), stop=(kc == KC - 1))
                return r
            S.op("pe", fn, ["win", skey], [("pG", bank)])

        load_x(0)
        S.dma("pool", load_wdn, "wdn", n=NF, writes=["wdn"])
        for b in range(nblocks):
            hk = [("hb", hslot(b, tt)) for tt in range(NT)]
            if b == 0:
                S.dma("pool", lambda e: e.dma_start(out=wout_v, in_=wout_d.rearrange("(e p) d -> p e d", p=128)),
                      "wout", reads=[], writes=[("ring", 0), ("ring", 1)])
                S.dma("sp", lambda e: e.dma_start(out=sout_d, in_=ring[:, :]), "woutst",
                      reads=[("ring", 0), ("ring", 1)], writes=["sout"])
            else:
                S.dma("sp", lambda e: e.dma_start(out=ring[:, :], in_=sout_d), "wout",
                      reads=["sout"], writes=[("ring", 0), ("ring", 1)])

            S.stage(1)
            rms_stats(b, 0, (0, 1))
            norm_transpose(b, 0, actA, "actA", C_NW1)

            S.stage(2)
            vtm = Ub[:, 18 * TB:22 * TB].rearrange("p (t c) -> p t c", t=NT)
            for tt in range(NT):
                bank = gbank(GM)

                def fn(e, bank=bank, tt=tt):
                    r = None
                    for kc in range(KC):
                        r = e.matmul(pG[bank][:, :], lhsT=actA[:, kc, tt * 128:(tt + 1) * 128], rhs=win[:, kc, 1024:1536],
                                     start=(kc == 0), stop=(kc == KC - 1))
                    return r
                S.op("pe", fn, ["win", "actA"], [("pG", bank)])
                COPY("act", vtm[:, tt, :], pG[bank][:, :], [("pG", bank)], [("U", 18 + tt)])

            S.stage(3)
            for hp in range(2):
                heads = (2 * hp, 2 * hp + 1)
                tl = {heads[0]: (0, 1, 2, 3), heads[1]: (4, 5, 6, 7)}
                for h in heads:
                    par = h % 2
                    t1, t2, t3, t4 = tl[h]
                    bq = gbank(GM)
                    proj_fm(h * 128, bq, actA, "actA")
                    ACT(T(t1), pG[bq][:, :], AF.Silu, [("pG", bq)], TK(t1))
                    bf_ = gbank(GM)
                    proj_fm(512 + h * 128, bf_, actA, "actA")
                    ACT(T(t2), pG[bf_][:, :], AF.Tanh, [("pG", bf_)], TK(t2), scale=0.5)
                    bg = gbank(GM)
                    proj_fm(1536 + h * 128, bg, actA, "actA")
                    ACT(sgs[:, par, :], pG[bg][:, :], AF.Silu, [("pG", bg)], [("sgs", par)])
                for h in heads:
                    par = h % 2
                    t1, t2, t3, t4 = tl[h]
                    ACT(T(t3), T(t2), AF.Ln, TK(t2) + ("dv_ab",), TK(t3), scale=dcolv(V_B + h), bias=dcolv(V_A + h))
                    TS(T(t2), T(t2), dcolv(V_NB + h), dcolv(V_B + h), ALU.mult, ALU.add, TK(t2) + ("dv_ab",), TK(t2))
                    SCAN(T(t4), scanmb[:, :], T(t3), 0.0, TK(t3) + ("scanmb",), TK(t4))
                    b3 = T(t4).rearrange("p (c j) -> p c j", j=64)
                    ACT(dcol[:, par, :], T(t4)[:, 63:TB:64], AF.Exp, TK(t4), [("dcol", par)])
                    TT(T(t3).rearrange("p (c j) -> p c j", j=64), b3[:, :, 63:64].to_broadcast([128, 8, 64]), b3,
                       ALU.subtract, TK(t4), TK(t3))
                    ACT(T(t4), T(t3), AF.Exp, TK(t3), TK(t4), scale=-1.0)
                    ACT(T(t3), T(t3), AF.Exp, TK(t3), TK(t3))
                    TT(QD[:, par, :], T(t1), T(t4), ALU.mult, TK(t1) + TK(t4), [("QD", par)])
                    TT(KD[:, par, :], T(t2), T(t3), ALU.mult, TK(t2) + TK(t3), [("KD", par)])
                S.stage(4)
                for h in heads:
                    par = h % 2

                    def trk(e, par=par):
                        r = None
                        for tt in range(NT):
                            r = e.transpose(pTR[:, tt * 128:(tt + 1) * 128], KD[:, par, tt * 128:(tt + 1) * 128], identb[:])
                        return r
                    S.op("pe", trk, [("KD", par), "identb"], ["pTR"])
                    trv = pTR[:, 0:TB].rearrange("p (t c) -> p t c", t=NT)
                    COPY("act", kdlo[0:64, par, :, :], trv[0:64, :, :], ["pTR"], [("kdlo", par)])
                    COPY("act", kdhi[64:128, par, :, :], trv[64:128, :, :], ["pTR"], [("kdhi", par)])

                    def sc(e, par=par):
                        r = None
                        for tt in range(NT):
                            r = e.matmul(pSC[:, tt, :], lhsT=KD[:, par, tt * 128:(tt + 1) * 128],
                                         rhs=QD[:, par, tt * 128:(tt + 1) * 128], start=True, stop=True)
                        return r
                    S.op("pe", sc, [("KD", par), ("QD", par)], ["pSC"])
                    TT(scm[:, par, :, :], pSC[:, :, :], trimb[:, :].unsqueeze(1).to_broadcast([128, NT, 128]), ALU.mult,
                       ["pSC", "trimb"], [("scm", par)])
                S.stage(5)
                for h in heads:
                    par = h % 2
                    ob = GO[par]
                    def pmm(e, par=par, h=h, lohalf=None):
                        r = None
                        for c in lohalf:
                            tt, half = c // 2, c % 2
                            kd = kdlo if half == 0 else kdhi
                            bankp = pPS if c < 4 else pSC
                            r = e.matmul(bankp[:, c % 4, :], lhsT=kd[:, par, tt, :], rhs=vtm[:, tt, h * 128:(h + 1) * 128],
                                         start=True, stop=True)
                        return r
                    S.op("pe", (lambda par, h: lambda e: pmm(e, par, h, range(0, 4)))(par, h),
                         [("kdlo", par), ("kdhi", par), ("U", 18), ("U", 19)], ["pPS"])
                    S.op("pe", (lambda par, h: lambda e: pmm(e, par, h, range(4, 8)))(par, h),
                         [("kdlo", par), ("kdhi", par), ("U", 20), ("U", 21)], ["pSC"])
                    for c in range(8):
                        tt, half = c // 2, c % 2
                        bankp = pPS if c < 4 else pSC
                        bkey = "pPS" if c < 4 else "pSC"
                        ACT(Spb[:, par, :], Sm[:, h, :], AF.Copy, [("Sm", h), ("dcol", par)], [("Spb", par)],
                            scale=dcol[:, par, c:c + 1])
                        STT(Sm[:, h, :], Sm[:, h, :], dcol[:, par, c:c + 1], bankp[:, c % 4, :], ALU.mult, ALU.add,
                            [("Sm", h), ("dcol", par), bkey], [("Sm", h)])

                        def om(e, par=par, tt=tt, half=half, c=c, h=h, ob=ob):
                            e.matmul(pG[ob][:, c * 64:(c + 1) * 64], lhsT=vtm[:, tt, h * 128:(h + 1) * 128],
                                     rhs=scm[:, par, tt, half * 64:(half + 1) * 64], start=True, stop=False)
                            return e.matmul(pG[ob][:, c * 64:(c + 1) * 64], lhsT=Spb[:, par, :],
                                            rhs=QD[:, par, c * 64:(c + 1) * 64], start=False, stop=True)
                        S.op("pe", om, [("U", 18 + tt), ("scm", par), ("Spb", par), ("QD", par)], [("pG", ob)])
                    S.stage(6)
                    t1, t2, t3, t4 = tl[h]
                    ACT(T(t1), pG[ob][:, :], AF.Square, [("pG", ob)], TK(t1))
                    bs = gbank(GM)
                    S.op("pe", (lambda bs, t1: lambda e: e.matmul(pG[bs][:, :], lhsT=onesf[:, :], rhs=T(t1), start=True, stop=True))(bs, t1),
                         ["onesf"] + list(TK(t1)), [("pG", bs)])
                    ACT(T(t2), pG[bs][:, :], AF.Ln, [("pG", bs), "dv_c"], TK(t2), bias=dcolv(V_EPS))
                    ACT(T(t2), T(t2), AF.Exp, TK(t2), TK(t2), scale=-0.5)
                    TT(T(t3), pG[ob][:, :], T(t2), ALU.mult, [("pG", ob)] + list(TK(t2)), TK(t3))
                    STT(actB[:, h, :], T(t3), col(C_HGNW), sgs[:, par, :], ALU.mult, ALU.mult,
                        TK(t3) + ("pv", ("sgs", par)), [("actB", h)])

            S.stage(7)
            for tp in range(2):
                tiles = (2 * tp, 2 * tp + 1)
                tl = {tiles[0]: (0, 1, 2, 3), tiles[1]: (4, 5, 6, 7)}
                for t in tiles:
                    par = t % 2
                    txc, tr_, ti, ta = tl[t]
                    bx_ = gbank(GM)
                    proj_fm(2048 + t * 128, bx_, actA, "actA")
                    COPY("dve", xbuf[:, par, 0:3], halo[:, t, :], ["halo"], [("xbuf", par)])
                    COPY("act", xbuf[:, par, 3:TB + 3], pG[bx_][:, :], [("pG", bx_)], [("xbuf", par)])
                    COPY("dve", halo[:, t, :], xbuf[:, par, TB:TB + 3], [("xbuf", par)], ["halo"])
                    TS(T(txc), xbuf[:, par, 3:TB + 3], col(C_CW + t * 4 + 3), col(C_CB + t), ALU.mult, ALU.add,
                       [("xbuf", par), "pv"], TK(txc))
                    for tap in range(3):
                        STT(T(txc), xbuf[:, par, tap:tap + TB], col(C_CW + t * 4 + tap), T(txc), ALU.mult, ALU.add,
                            [("xbuf", par), "pv"] + list(TK(txc)), TK(txc))
                    COPY("act", xcb[:, par, :], T(txc), TK(txc), [("xcb", par)])
                    br = gbank(GM)
                    S.op("pe", (lambda br, t, par: lambda e: e.matmul(pG[br][:, :], lhsT=wab[:, t, :], rhs=xcb[:, par, :], start=True, stop=True))(br, t, par),
                         ["wab", ("xcb", par)], [("pG", br)])
                    ACT(T(tr_), pG[br][:, :], AF.Tanh, [("pG", br), "dv_lru"], TK(tr_), scale=0.5, bias=dcolv(V_HBA + t))
                    bi = gbank(GM)
                    S.op("pe", (lambda bi, t, par: lambda e: e.matmul(pG[bi][:, :], lhsT=wxb[:, t, :], rhs=xcb[:, par, :], start=True, stop=True))(bi, t, par),
                         ["wxb", ("xcb", par)], [("pG", bi)])
                    ACT(T(ti), pG[bi][:, :], AF.Tanh, [("pG", bi), "dv_lru"], TK(ti), scale=0.5, bias=dcolv(V_HBX + t))
                    STT(T(ti), T(ti), 1.0, T(txc), ALU.add, ALU.mult, TK(ti) + TK(txc), TK(ti))
                for t in tiles:
                    txc, tr_, ti, ta = tl[t]
                    ACT(T(ta), T(tr_), AF.Exp, TK(tr_) + ("dv_lru",), TK(ta), scale=dcolv(V_HC + t), bias=dcolv(V_HC + t))
                    ACT(T(tr_), T(tr_), AF.Exp, TK(tr_) + ("dv_lru",), TK(tr_), scale=dcolv(V_C + t), bias=dcolv(V_C + t))
                    ACT(T(tr_), T(tr_), AF.Ln, TK(tr_) + ("dv_c",), TK(tr_), scale=-1.0, bias=dcolv(V_ONE))
                    ACT(T(tr_), T(tr_), AF.Exp, TK(tr_) + ("dv_c",), TK(tr_), scale=0.5, bias=dcolv(V_LN05))
                    TT(T(ti), T(ti), T(tr_), ALU.mult, TK(ti) + TK(tr_), TK(ti))
                    SCAN(T(tr_), T(ta), T(ti), hcar[:, t:t + 1], TK(ta) + TK(ti) + ("hcar",), TK(tr_))
                    COPY("dve", hcar[:, t:t + 1], T(tr_)[:, TB - 1:TB], TK(tr_), ["hcar"])
                for t in tiles:
                    txc, tr_, ti, ta = tl[t]
                    bgt = gbank(GM)
                    proj_fm(2560 + t * 128, bgt, actA, "actA")
                    ACT(T(txc), pG[bgt][:, :], AF.Gelu_apprx_tanh, [("pG", bgt)], TK(txc))
                    TT(actB[:, 4 + t, :], T(tr_), T(txc), ALU.mult, TK(tr_) + TK(txc), [("actB", 4 + t)])

            S.stage(8)
            akeys = [("actB", i) for i in range(8)]
            for tt in range(NT):
                s = hslot(b, tt)
                for dh in range(2):
                    bank = gbank(GF)

                    def fn(e, bank=bank, tt=tt, dh=dh):
                        r = None
                        for ec in range(KC):
                            r = e.matmul(pG[bank][:, :], lhsT=actB[:, ec, tt * 128:(tt + 1) * 128],
                                         rhs=wout_v[:, ec, dh * 512:(dh + 1) * 512], start=(ec == 0), stop=(ec == KC - 1))
                        return r
                    S.op("pe", fn, akeys + [("ring", 0), ("ring", 1)], [("pG", bank)])
                    TT(hb[:, s, dh * 512:(dh + 1) * 512], pG[bank][:, :], hb[:, s, dh * 512:(dh + 1) * 512], ALU.add,
                       [("pG", bank), ("hb", s)], [("hb", s)])

            S.stage(9)
            rms_stats(b, 4, (0, 1))
            norm_transpose(b, 4, actA, "actA", C_NW2)

            if b + 1 < nblocks:
                load_x(b + 1, [0])
            S.stage(10)
            for g in range(NG):
                sl = g % 2
                rs = ring_slot(sl)
                if b == 0:
                    def ld(e, g=g, rs=rs):
                        a = e.dma_start(out=rs[:, :, 0:256], in_=wgu_d[:, g * 256:(g + 1) * 256].rearrange("(k p) c -> p k c", p=128))
                        bb = e.dma_start(out=rs[:, :, 256:512], in_=wgu_d[:, DFF + g * 256:DFF + (g + 1) * 256].rearrange("(k p) c -> p k c", p=128))
                        return [a, bb]
                    S.dma("pool", ld, ("rl", sl), n=2, writes=[("ring", sl)])
                    S.dma("sp", (lambda g, sl: lambda e: e.dma_start(out=sgu_d[g], in_=ring[:, sl * 4096:(sl + 1) * 4096]))(g, sl),
                          ("rst", sl), reads=[("ring", sl)], writes=[("sgu", g)])
                else:
                    S.dma("sp", (lambda g, sl: lambda e: e.dma_start(out=ring[:, sl * 4096:(sl + 1) * 4096], in_=sgu_d[g]))(g, sl),
                          ("rl", sl), reads=[("sgu", g)], writes=[("ring", sl)])
                for j in range(2):
                    f = 2 * g + j
                    bgate = gbank(GF)
                    bup = gbank(GF)

                    def mm(e, rs=rs, j=j, bgate=bgate, bup=bup):
                        r = None
                        for kc in range(KC):
                            r = e.matmul(pG[bgate][:, :], lhsT=rs[:, kc, j * 128:(j + 1) * 128], rhs=actA[:, kc, :],
                                         start=(kc == 0), stop=(kc == KC - 1))
                        for kc in range(KC):
                            r = e.matmul(pG[bup][:, :], lhsT=rs[:, kc, 256 + j * 128:256 + (j + 1) * 128], rhs=actA[:, kc, :],
                                         start=(kc == 0), stop=(kc == KC - 1))
                        return r
                    S.op("pe", mm, [("ring", sl), "actA"], [("pG", bgate), ("pG", bup)])
                    sp_ = f % 2
                    ACT(sgt[:, sp_, :], pG[bgate][:, :], AF.Silu, [("pG", bgate)], [("sgt", sp_)])
                    TT(Ub[:, f * TB:(f + 1) * TB], pG[bup][:, :], sgt[:, sp_, :], ALU.mult, [("pG", bup), ("sgt", sp_)], [("U", f)])

            S.stage(11)
            ukeys = [("U", f) for f in range(NF)]
            for tt in range(NT):
                s = hslot(b, tt)
                for dh in range(2):
                    bank = gbank(GF)

                    def fn(e, bank=bank, tt=tt, dh=dh):
                        r = None
                        for f in range(NF):
                            r = e.matmul(pG[bank][:, :], lhsT=Ub[:, f * TB + tt * 128:f * TB + (tt + 1) * 128],
                                         rhs=wdn[:, f, dh * 512:(dh + 1) * 512], start=(f == 0), stop=(f == NF - 1))
                        return r
                    S.op("pe", fn, ukeys + ["wdn"], [("pG", bank)])
                    TT(hb[:, s, dh * 512:(dh + 1) * 512], pG[bank][:, :], hb[:, s, dh * 512:(dh + 1) * 512], ALU.add,
                       [("pG", bank), ("hb", s)], [("hb", s)])

            S.stage(12)
            for tt in range(NT):
                s = hslot(b, tt)
                ACT(sgt[:, 0, :].bitcast(BF16), hb[:, s, :], AF.Square, [("hb", s)], [("sgt", 0), ("ssq", 8 + tt)],
                    accum=ssq[:, 8 + tt:9 + tt])
            ks = [("ssq", 8 + tt) for tt in range(NT)]
            kr = [("rstd", 8 + tt) for tt in range(NT)]
            ACT(rstd[:, 8:12], ssq[:, 8:12], AF.Ln, ks + ["dv_c"], kr, scale=1.0 / D, bias=dcolv(V_EPS))
            ACT(rstd[:, 8:12], rstd[:, 8:12], AF.Exp, kr, kr, scale=-0.5)
            for tt in range(NT):
                s = hslot(b, tt)
                r0 = b * TB + tt * 128
                STT(hb[:, s, :], hb[:, s, :], rstd[:, 8 + tt:9 + tt], finw[:, :], ALU.mult, ALU.mult,
                    [("hb", s), ("rstd", 8 + tt), "finw"], [("hb", s)])
                S.dma("sp", (lambda s, r0: lambda e: e.dma_start(out=out_d[r0:r0 + 128, :], in_=hb[:, s, :]))(s, r0),
                      ("ost", s), reads=[("hb", s)], writes=[("hb", s), ("OUT", s)])
            if b + 1 < nblocks:
                load_x(b + 1, [1, 2, 3])

        S.final_wait("sp", [("OUT", s) for s in range(5)])
        S.emit()
    return nc


def _host_consts():
    ident = np.eye(128, dtype=np.float32)
    s = np.arange(128)[:, None]
    t = np.arange(128)[None, :]
    trimask = ((s // 64 == t // 64) & (s <= t)).astype(np.float32)
    scanmask = np.ones((1, TB), np.float32)
    scanmask[0, ::64] = 0.0
    onesm = np.full((128, 128), 1.0 / 128.0, np.float32)
    return ident, trimask, scanmask, onesm


def _pack_vec(v):
    v = np.asarray(v, np.float32).reshape(-1)
    return v.reshape(-1, 128).T


_CACHE = {}


def kernel(x, mix_norm_w, w_in, hg_lb, hg_norm_w, conv_w, conv_b, lru_wa, lru_ba,
           lru_wx, lru_bx, lru_a, w_out, ffn_norm_w, w_gate_up, w_down, final_norm_w):
    x = np.asarray(x, np.float32)
    B = x.shape[0]
    ident, trimask, scanmask, onesm = _host_consts()
    pvec = np.zeros((128, NCOL), np.float32)
    pvec[:, C_NW1:C_NW1 + 8] = _pack_vec(mix_norm_w[0])
    pvec[:, C_NW2:C_NW2 + 8] = _pack_vec(ffn_norm_w[0])
    pvec[:, C_A0:C_A0 + 4] = _pack_vec(hg_lb[0])
    pvec[:, C_A1:C_A1 + 4] = _pack_vec(hg_lb[1])
    pvec[:, C_HGNW:C_HGNW + 1] = _pack_vec(hg_norm_w[0])
    cw = np.asarray(conv_w[0], np.float32)
    for t in range(4):
        for tap in range(4):
            pvec[:, C_CW + t * 4 + tap] = cw[tap, t * 128:(t + 1) * 128]
    pvec[:, C_CB:C_CB + 4] = _pack_vec(conv_b[0])
    pvec[:, C_BA:C_BA + 4] = _pack_vec(lru_ba[0])
    pvec[:, C_BX:C_BX + 4] = _pack_vec(lru_bx[0])
    pvec[:, C_LA:C_LA + 4] = _pack_vec(lru_a[0])
    wabd = np.zeros((128, 4, 128), np.float32)
    wxbd = np.zeros((128, 4, 128), np.float32)
    wa = np.asarray(lru_wa[0], np.float32)
    wx = np.asarray(lru_wx[0], np.float32)
    for t in range(4):
        for j in range(2):
            wabd[j * 64:(j + 1) * 64, t, j * 64:(j + 1) * 64] = wa[2 * t + j]
            wxbd[j * 64:(j + 1) * 64, t, j * 64:(j + 1) * 64] = wx[2 * t + j]
    shared = {
        "w_in": np.ascontiguousarray(w_in[0], np.float32),
        "w_out": np.ascontiguousarray(w_out[0], np.float32),
        "w_gu": np.ascontiguousarray(w_gate_up[0], np.float32),
        "w_down": np.ascontiguousarray(w_down[0], np.float32),
        "pvec": pvec, "wabd": wabd, "wxbd": wxbd,
        "finw": np.asarray(final_norm_w, np.float32).reshape(1, D),
        "ident": ident, "trimask": trimask, "scanmask": scanmask, "onesm": onesm,
    }
    if "nc" not in _CACHE:
        _CACHE["nc"] = build_program()
    nc = _CACHE["nc"]
    in_maps = []
    for c in range(B):
        m = dict(shared)
        m["x"] = np.ascontiguousarray(x[c])
        in_maps.append(m)
    res = run_bass_kernel_spmd(nc, in_maps, core_ids=list(range(B)))
    return np.stack([np.asarray(r["out"], np.float32) for r in res.results], axis=0)
```

```python
import math
import numpy as np
from contextlib import ExitStack
import concourse.bass as bass
import concourse.mybir as mybir
from concourse.bass_utils import run_bass_kernel_spmd

F32 = mybir.dt.float32
BF16 = mybir.dt.bfloat16
AF = mybir.ActivationFunctionType
ALU = mybir.AluOpType

ENGS = ("pe", "act", "dve", "pool", "sp")

D = 1024
SEQ = 4096
TB = 512
NB = SEQ // TB
NT = TB // 128
KC = D // 128
DIN = 3072
DFF = 2816
NF = DFF // 128
NG = NF // 2
EPS = 1e-6
NCOL = 57
C_NW1, C_NW2, C_A0, C_A1, C_HGNW, C_CW, C_CB, C_BA, C_BX, C_LA = 0, 8, 16, 20, 24, 25, 41, 45, 49, 53
V_A, V_B, V_NB, V_HBA, V_HBX, V_C, V_HC, V_EPS, V_ONE, V_LN05, V_TMP = 0, 4, 8, 12, 16, 20, 24, 28, 29, 30, 32
NDV = 48


class Sched:
    XLAT = 120.0
    TBL = 2700.0
    WIN = 150.0
    COAL = 16
    COAL_MARGIN = 300.0

    def __init__(self, nc):
        self.nc = nc
        self.ops = []
        self.lw = {}
        self.rd = {}
        self.cut = False

    def stage(self, k):
        pass

    @staticmethod
    def _is_ps(k):
        n = k[0] if isinstance(k, tuple) else k
        return n in ("pF", "pM", "pO", "pX")

    def _add(self, eng, fn, reads, writes, dur, dsem=None, n=0, aset=None, lat=0.0):
        if self.cut:
            return
        writes = tuple(writes) + tuple(k for k in reads if self._is_ps(k))
        reads = tuple(k for k in reads if not self._is_ps(k))
        idx = len(self.ops)
        preds = set()
        for r in reads:
            p = self.lw.get(r)
            if p is not None:
                preds.add(p)
        for w in writes:
            p = self.lw.get(w)
            if p is not None:
                preds.add(p)
            preds.update(self.rd.get(w, ()))
        preds.discard(idx)
        for r in reads:
            self.rd.setdefault(r, []).append(idx)
        for w in writes:
            self.lw[w] = idx
            self.rd[w] = []
        import sys as _s
        f = _s._getframe(2)
        lines = []
        while f is not None and len(lines) < 3:
            lines.append(f.f_lineno)
            f = f.f_back
        self.ops.append(dict(eng=eng, fn=fn, preds=sorted(preds), dur=float(dur), dsem=dsem, n=n, aset=aset, lat=float(lat), lines=lines))

    def op(self, eng, fn, reads=(), writes=(), dur=500.0, aset=None):
        self._add(eng, fn, reads, writes, dur, aset=aset)

    def dma(self, eng, fn, sem, n=1, reads=(), writes=(), nbytes=1 << 19):
        self._add(eng, fn, reads, writes, 60.0 * n, dsem=("dma", sem), n=n, lat=2000.0 + nbytes / 150.0)

    def final_wait(self, eng, keys):
        self._add(eng, None, (), tuple(keys), 10.0)

    def schedule(self):
        ops = self.ops
        n = len(ops)
        succs = [[] for _ in range(n)]
        indeg = [0] * n
        for i, o in enumerate(ops):
            indeg[i] = len(o["preds"])
            for p in o["preds"]:
                succs[p].append(i)
        tail = [0.0] * n
        for i in range(n - 1, -1, -1):
            t = 0.0
            for s_ in succs[i]:
                if tail[s_] > t:
                    t = tail[s_]
            tail[i] = t + ops[i]["dur"] + ops[i]["lat"]
        ready = {e: [] for e in ENGS}
        rtime = [0.0] * n
        fin = [0.0] * n
        for i in range(n):
            if indeg[i] == 0:
                ready[ops[i]["eng"]].append(i)
        free = {e: 0.0 for e in ENGS}
        cur_set = [None]
        order = {e: [] for e in ENGS}
        done = 0
        while done < n:
            best = None
            for e in ENGS:
                rl = ready[e]
                if not rl:
                    continue
                st_min = None
                for i in rl:
                    st = max(free[e], rtime[i])
                    if e == "act" and ops[i]["aset"] is not None and cur_set[0] is not None and ops[i]["aset"] != cur_set[0]:
                        st += self.TBL
                    if st_min is None or st < st_min:
                        st_min = st
                cand = None
                for i in rl:
                    st = max(free[e], rtime[i])
                    if e == "act" and ops[i]["aset"] is not None and cur_set[0] is not None and ops[i]["aset"] != cur_set[0]:
                        st += self.TBL
                    if st <= st_min + self.WIN:
                        key = (-tail[i], i)
                        if cand is None or key < cand[0]:
                            cand = (key, i, st)
                if best is None or cand[2] < best[2]:
                    best = (e, cand[1], cand[2])
            e, i, st = best
            o = ops[i]
            ready[e].remove(i)
            if e == "act" and o["aset"] is not None:
                cur_set[0] = o["aset"]
            o["bind"] = ("eng", order[e][-1] if order[e] else None) if free[e] >= rtime[i] else ("dep", max(o["preds"], key=lambda p: fin[p]) if o["preds"] else None)
            o["start"] = st
            free[e] = st + o["dur"]
            fin[i] = st + o["dur"] + o["lat"]
            order[e].append(i)
            done += 1
            for s_ in succs[i]:
                t = fin[i] + (self.XLAT if ops[s_]["eng"] != e else 0.0)
                if t > rtime[s_]:
                    rtime[s_] = t
                indeg[s_] -= 1
                if indeg[s_] == 0:
                    ready[ops[s_]["eng"]].append(s_)
        self.order = order
        self.model_time = max(fin) if n else 0.0
        return order

    def emit(self):
        nc = self.nc
        ops = self.ops
        order = self.schedule()
        tok = [None] * len(ops)
        dcount = {}
        for e in ENGS:
            c = 0
            for i in order[e]:
                o = ops[i]
                if o["fn"] is None:
                    continue
                if o["dsem"] is None:
                    c += 1
                    tok[i] = (e, c)
                else:
                    dcount[o["dsem"]] = dcount.get(o["dsem"], 0) + o["n"]
                    tok[i] = (o["dsem"], 16 * dcount[o["dsem"]])
        with ExitStack() as st:
            sems = {}
            for e in ENGS:
                sems[e] = st.enter_context(nc.semaphore("s_" + e))
            for k, sk in enumerate(dcount):
                sems[sk] = st.enter_context(nc.semaphore("d%d" % k))
            block = st.enter_context(nc.Block())

            fin_of = {}
            for i, o in enumerate(ops):
                if tok[i] is not None:
                    fin_of[tok[i]] = o["start"] + o["dur"] + o["lat"]

            def needs_of(e_name, i):
                need = {}
                for p in ops[i]["preds"]:
                    t = tok[p]
                    if t is None:
                        continue
                    sk, v = t
                    if sk == e_name and e_name == "pe":
                        continue
                    if need.get(sk, 0) < v:
                        need[sk] = v
                return need

            def run(e_name, e):
                waited = {}
                seq = order[e_name]
                needs = [needs_of(e_name, i) for i in seq]
                for j, i in enumerate(seq):
                    o = ops[i]
                    for sk, v in needs[j].items():
                        if waited.get(sk, 0) >= v:
                            continue
                        for jj in range(j + 1, min(j + 1 + self.COAL, len(seq))):
                            v2 = needs[jj].get(sk, 0)
                            if v2 > v and fin_of.get((sk, v2), 1e30) + self.COAL_MARGIN <= o["start"]:
                                v = v2
                        waited[sk] = v
                        e.wait_ge(sems[sk], v)
                    if o["fn"] is None:
                        continue
                    r = o["fn"](e)
                    if o["dsem"] is None:
                        if isinstance(r, (list, tuple)):
                            r = r[-1]
                        r.then_inc(sems[e_name], 1)
                    else:
                        if not isinstance(r, (list, tuple)):
                            r = [r]
                        assert len(r) == o["n"], (len(r), o["n"])
                        for ins in r:
                            ins.then_inc(sems[o["dsem"]], 16)
                self.nwaits = getattr(self, "nwaits", {})
                self.nwaits[e_name] = sum(1 for _ in ())

            @block.tensor
            def _(e):
                run("pe", e)

            @block.scalar
            def _(e):
                run("act", e)

            @block.vector
            def _(e):
                run("dve", e)

            @block.gpsimd
            def _(e):
                run("pool", e)

            @block.sync
            def _(e):
                run("sp", e)


class WStream:
    def __init__(self, ns, la, plan=None):
        self.ns = ns
        self.la = la
        self.dry = plan is None
        self.plan = [] if plan is None else plan
        self.i = 0
        self.emitted = 0

    def get(self, spec, emit_load):
        i = self.i
        self.i += 1
        if self.dry:
            self.plan.append(spec)
            return i % self.ns
        assert self.plan[i] == spec, (i, self.plan[i], spec)
        hi = min(i + self.la, len(self.plan) - 1)
        while self.emitted <= hi:
            emit_load(self.plan[self.emitted], self.emitted % self.ns)
            self.emitted += 1
        return i % self.ns


def build_program(nblocks=NB):
    nc = bass.Bass("TRN2", target_bir_lowering=False)

    def din(name, shape, dt=F32):
        return nc.dram_tensor(name, shape, dt, kind="ExternalInput").ap()

    x_d = din("x", [SEQ, D])
    win_d = din("w_in", [D, DIN])
    wout_d = din("w_out", [D, D])
    wgu_d = din("w_gu", [D, 2 * DFF])
    wdn_d = din("w_down", [DFF, D])
    pvec_d = din("pvec", [128, NCOL])
    wabd_d = din("wabd", [128, 4, 128])
    wxbd_d = din("wxbd", [128, 4, 128])
    finw_d = din("finw", [1, D])
    ident_d = din("ident", [128, 128])
    trim_d = din("trimask", [128, 128])
    scanm_d = din("scanmask", [1, TB])
    ones_d = din("onesm", [128, 128])
    out_d = nc.dram_tensor("out", [SEQ, D], F32, kind="ExternalOutput").ap()
    scr = {
        "s_in": nc.dram_tensor("s_in", [6, 128, 4096], BF16, kind="Internal").ap(),
        "s_gu": nc.dram_tensor("s_gu", [NG, 128, 4096], BF16, kind="Internal").ap(),
        "s_dn": nc.dram_tensor("s_dn", [6, 128, 4096], BF16, kind="Internal").ap(),
    }

    with ExitStack() as st:
        def sb(name, shape, dt=F32):
            return st.enter_context(nc.sbuf_tensor(name, shape, dt))

        def ps(name, shape, dt=F32):
            return st.enter_context(nc.psum_tensor(name, shape, dt))

        wrF = sb("wrF", [128, 3, 4096], BF16)
        wrM = sb("wrM", [128, 3, 4096], BF16)
        wout = sb("wout", [128, KC, D], BF16)
        hb = sb("hb", [128, 8, D], F32)
        actA = sb("actA", [128, 2, KC, TB], BF16)
        actB = sb("actB", [128, KC, TB], BF16)
        U = sb("U", [128, NF * TB // 2], F32)
        Ub = U.bitcast(BF16)
        M = sb("M", [128, 14 * TB], F32)
        Mb = M.bitcast(BF16)
        sgt = sb("sgt", [128, 2, TB], F32)
        kdlo = sb("kdlo", [128, 2, NT, 128], BF16)
        kdhi = sb("kdhi", [128, 2, NT, 128], BF16)
        QD = sb("QD", [128, 2, TB], BF16)
        KD = sb("KD", [128, 2, TB], BF16)
        scm = sb("scm", [128, 2, NT, 128], BF16)
        sgs = sb("sgs", [128, 4, TB], BF16)
        Sring = sb("Sring", [128, 8, 128], F32)
        Sm = sb("Sm", [128, 4, 128], F32)
        dcol = sb("dcol", [128, 2, 8], F32)
        xbuf = sb("xbuf", [128, 1, TB + 3], F32)
        halo = sb("halo", [128, 4, 3], F32)
        hcar = sb("hcar", [128, 4], F32)
        xcb = sb("xcb", [128, 1, TB], BF16)
        identb = sb("identb", [128, 128], BF16)
        trimb = sb("trimb", [128, 128], BF16)
        scanmb = sb("scanmb", [128, TB], BF16)
        onesf = sb("onesf", [128, 128], F32)
        finw = sb("finw_s", [128, D], F32)
        wab = sb("wab", [128, 4, 128], BF16)
        wxb = sb("wxb", [128, 4, 128], BF16)
        pv = sb("pv", [128, NCOL], F32)
        dv = sb("dv", [128, NDV], F32)
        ssq = sb("ssq", [128, 16], F32)
        rstd = sb("rstd", [128, 16], F32)

        pF = [ps("pf%d" % i, [128, TB], F32) for i in range(4)]
        pM = [ps("pm%d" % i, [128, TB], F32) for i in range(2)]
        pO = ps("po", [128, TB], F32)
        pX = ps("px", [128, TB], F32)
        pXb = pX.bitcast(BF16)

        ctx = {}

        def emit_all(S, WF, WM):
            st8 = {"m": 0, "f": 0, "set": "S", "nm": 0, "nf": 0}

            def T(i):
                return M[:, i * TB:(i + 1) * TB]

            def TK(i):
                return (("M", i),)

            def ACT(out, in_, func, reads, writes, scale=1.0, bias=None, accum=None):
                if func in (AF.Silu, AF.Tanh):
                    st8["set"] = "S"
                elif func in (AF.Ln, AF.Exp):
                    st8["set"] = "L"

                def fn(e):
                    kw = dict(out=out, in_=in_, func=func, scale=scale)
                    if bias is not None:
                        kw["bias"] = bias
                    if accum is not None:
                        kw["accum_out"] = accum
                    return e.activation(**kw)
                aset = "S" if func in (AF.Silu, AF.Tanh) else ("L" if func in (AF.Ln, AF.Exp) else None)
                S.op("act", fn, reads, writes, dur=220.0 + 0.833 * in_.free_size() + (100.0 if accum is not None else 0.0), aset=aset)

            def TS(out, in0, s1, s2, op0, op1, reads, writes, eng="dve"):
                def fn(e):
                    if s2 is None:
                        return e.tensor_scalar(out=out, in0=in0, scalar1=s1, scalar2=None, op0=op0)
                    return e.tensor_scalar(out=out, in0=in0, scalar1=s1, scalar2=s2, op0=op0, op1=op1)
                S.op(eng, fn, reads, writes, dur=150.0 + 0.7 * out.free_size())

            def TT(out, in0, in1, op, reads, writes, eng="dve"):
                S.op(eng, lambda e: e.tensor_tensor(out=out, in0=in0, in1=in1, op=op), reads, writes, dur=150.0 + 1.04 * out.free_size())

            def STT(out, in0, scalar, in1, op0, op1, reads, writes):
                S.op("dve", lambda e: e.scalar_tensor_tensor(out=out, in0=in0, scalar=scalar, in1=in1, op0=op0, op1=op1), reads, writes, dur=150.0 + 1.04 * out.free_size())

            def SCAN(out, d0, d1, init, reads, writes):
                S.op("dve", lambda e: e.tensor_tensor_scan(out=out, data0=d0, data1=d1, initial=init, op0=ALU.mult, op1=ALU.add), reads, writes, dur=150.0 + 2.1 * out.free_size())

            def COPY(eng, out, in_, reads, writes):
                if eng == "act":
                    ACT(out, in_, AF.Copy, reads, writes)
                else:
                    S.op(eng, lambda e: e.tensor_copy(out=out, in_=in_), reads, writes, dur=150.0 + 0.6 * out.free_size())

            def MEMSET(eng, ap, val, writes):
                S.op(eng, lambda e: e.memset(ap, val), (), writes, dur=200.0)

            def mbank():
                i = st8["m"] % 2
                st8["m"] += 1
                return i

            def col(c):
                return pv[:, c:c + 1]

            def dcolv(c):
                return dv[:, c:c + 1]

            def mk_load(ring, rk):
                def emit_load(spec, slot):
                    name, idx = spec
                    S.dma("sp", lambda e: e.dma_start(out=ring[:, slot, :], in_=scr[name][idx]), (rk, slot),
                          reads=[(name, idx)], writes=[(rk, slot)], nbytes=1 << 20)
                return emit_load
            loadF = mk_load(wrF, "wrF")
            loadM = mk_load(wrM, "wrM")

            def wget(name, idx):
                if name == "s_in":
                    slot = WM.get((name, idx), loadM)
                    return ("wrM", slot), wrM[:, slot, :].rearrange("p (k c) -> p k c", k=8)
                slot = WF.get((name, idx), loadF)
                return ("wrF", slot), wrF[:, slot, :].rearrange("p (k c) -> p k c", k=8)

            def load_x(b):
                for tt in range(NT):
                    s = (b % 2) * 4 + tt
                    r0 = b * TB + tt * 128
                    S.dma("pool", (lambda s, r0: lambda e: e.dma_start(out=hb[:, s, :], in_=x_d[r0:r0 + 128, :]))(s, r0),
                          ("hb", s), writes=[("hb", s)])

            S.dma("sp", lambda e: e.dma_start(out=pv[:], in_=pvec_d), "pv", writes=["pv"])
            S.dma("sp", lambda e: e.dma_start(out=onesf[:], in_=ones_d), "onesf", writes=["onesf"])
            S.dma("sp", lambda e: e.dma_start(out=finw[:], in_=finw_d.partition_broadcast(128)), "finw", writes=["finw"])
            load_x(0)
            S.dma("pool", lambda e: e.dma_start(out=identb[:], in_=ident_d), "identb", writes=["identb"])
            S.dma("pool", lambda e: e.dma_start(out=trimb[:], in_=trim_d), "trimb", writes=["trimb"])
            S.dma("pool", lambda e: e.dma_start(out=scanmb[:], in_=scanm_d.partition_broadcast(128)), "scanmb", writes=["scanmb"])
            S.dma("pool", lambda e: e.dma_start(out=wab[:], in_=wabd_d), "wab", writes=["wab"])
            S.dma("pool", lambda e: e.dma_start(out=wxb[:], in_=wxbd_d), "wxb", writes=["wxb"])

            def img(name, idx):
                return scr[name][idx].rearrange("p (k c) -> p k c", k=8)

            cast_hist = []

            def cast_dep():
                return [cast_hist[-3]] if len(cast_hist) >= 3 else []

            def cast_in(idx, pieces):
                def fn(e):
                    r = []
                    for j, (c0, w) in enumerate(pieces):
                        dst = img("s_in", idx)[:, :, j * w:(j + 1) * w]
                        r.append(e.dma_start(out=dst, in_=win_d[:, c0:c0 + w].rearrange("(k p) c -> p k c", p=128)))
                    return r
                S.dma("pool", fn, ("s_in", idx), n=len(pieces), reads=cast_dep(), writes=[("s_in", idx)], nbytes=3 << 20)
                cast_hist.append(("s_in", idx))

            cast_in(0, [(1024, 512)])
            cast_in(3, [(1536, 512)])
            cast_in(1, [(0, 128), (512, 128), (128, 128), (640, 128)])
            cast_in(2, [(256, 128), (768, 128), (384, 128), (896, 128)])
            cast_in(4, [(2048, 512)])
            cast_in(5, [(2560, 512)])
            S.dma("pool", lambda e: e.dma_start(out=wout[:], in_=wout_d.rearrange("(e p) d -> p e d", p=128)), "wout",
                  reads=cast_dep(), writes=["wout"], nbytes=6 << 20)
            cast_hist.append("wout")

            def cast_gu(g):
                def fn(e):
                    a = e.dma_start(out=img("s_gu", g)[:, :, 0:256], in_=wgu_d[:, g * 256:(g + 1) * 256].rearrange("(k p) c -> p k c", p=128))
                    bb = e.dma_start(out=img("s_gu", g)[:, :, 256:512], in_=wgu_d[:, DFF + g * 256:DFF + (g + 1) * 256].rearrange("(k p) c -> p k c", p=128))
                    return [a, bb]
                S.dma("pool", fn, ("s_gu", g), n=2, reads=cast_dep(), writes=[("s_gu", g)], nbytes=3 << 20)
                cast_hist.append(("s_gu", g))

            FG = [(0, 8), (8, 8), (14, 8)]

            def cast_dn(j):
                dh, fg = j // 3, j % 3
                f0, nf = FG[fg]
                S.dma("pool", lambda e: e.dma_start(out=img("s_dn", j)[:, 0:nf, :],
                                                    in_=wdn_d[f0 * 128:(f0 + nf) * 128, dh * 512:(dh + 1) * 512].rearrange("(f p) c -> p f c", p=128)),
                      ("s_dn", j), reads=cast_dep(), writes=[("s_dn", j)], nbytes=3 << 20)
                cast_hist.append(("s_dn", j))

            for g in range(NG):
                cast_gu(g)
            for j in range(6):
                cast_dn(j)

            MEMSET("dve", Sm[:], 0.0, [("Sm", h) for h in range(4)])
            MEMSET("dve", halo[:], 0.0, ["halo"])
            MEMSET("dve", hcar[:], 0.0, ["hcar"])
            MEMSET("dve", kdlo[:], 0.0, [("kdlo", 0), ("kdlo", 1)])
            MEMSET("dve", kdhi[:], 0.0, [("kdhi", 0), ("kdhi", 1)])
            MEMSET("dve", dv[:, V_EPS:V_EPS + 1], EPS, ["dv_c"])
            MEMSET("dve", dv[:, V_ONE:V_ONE + 1], 1.0, ["dv_c"])
            MEMSET("dve", dv[:, V_LN05:V_LN05 + 1], math.log(0.5), ["dv_c"])
            TT(dv[:, V_TMP:V_TMP + 4], pv[:, C_A0:C_A0 + 4], pv[:, C_A1:C_A1 + 4], ALU.subtract, ["pv"], ["dv_t"])
            ACT(dv[:, V_TMP:V_TMP + 4], dv[:, V_TMP:V_TMP + 4], AF.Tanh, ["dv_t"], ["dv_t"], scale=0.5)
            TS(dv[:, V_A:V_A + 4], dv[:, V_TMP:V_TMP + 4], 0.25, 0.75, ALU.mult, ALU.add, ["dv_t"], ["dv_ab"])
            TS(dv[:, V_B:V_B + 4], dv[:, V_TMP:V_TMP + 4], -0.25, 0.25, ALU.mult, ALU.add, ["dv_t"], ["dv_ab"])
            TS(dv[:, V_NB:V_NB + 4], dv[:, V_TMP:V_TMP + 4], 0.25, -0.25, ALU.mult, ALU.add, ["dv_t"], ["dv_ab"])
            TS(dv[:, V_HBA:V_HBA + 4], pv[:, C_BA:C_BA + 4], 0.5, None, ALU.mult, None, ["pv"], ["dv_lru"])
            TS(dv[:, V_HBX:V_HBX + 4], pv[:, C_BX:C_BX + 4], 0.5, None, ALU.mult, None, ["pv"], ["dv_lru"])
            ACT(dv[:, V_TMP + 4:V_TMP + 8], pv[:, C_LA:C_LA + 4], AF.Exp, ["pv"], ["dv_t2"], scale=-1.0)
            ACT(dv[:, V_TMP + 4:V_TMP + 8], dv[:, V_TMP + 4:V_TMP + 8], AF.Ln, ["dv_t2", "dv_c"], ["dv_t2"], bias=dcolv(V_ONE))
            TS(dv[:, V_C:V_C + 4], dv[:, V_TMP + 4:V_TMP + 8], -8.0, None, ALU.mult, None, ["dv_t2"], ["dv_lru"])
            TS(dv[:, V_HC:V_HC + 4], dv[:, V_TMP + 4:V_TMP + 8], -4.0, None, ALU.mult, None, ["dv_t2"], ["dv_lru"])

            def rms_stats(slots, base, junk):
                for tt in range(NT):
                    s = slots[tt]
                    jk = junk[tt % 2]
                    ACT(Mb[:, jk * 1024:(jk + 1) * 1024], hb[:, s, :], AF.Square, [("hb", s)],
                        [("M", jk), ("ssq", base + tt)], accum=ssq[:, base + tt:base + tt + 1])
                ks = [("ssq", base + tt) for tt in range(NT)]
                kr = [("rstd", base + tt) for tt in range(NT)]
                ACT(rstd[:, base:base + 4], ssq[:, base:base + 4], AF.Ln, ks + ["dv_c"], kr, scale=1.0 / D, bias=dcolv(V_EPS))
                ACT(rstd[:, base:base + 4], rstd[:, base:base + 4], AF.Exp, kr, kr, scale=-0.5)

            def norm_transpose(slots, base, dst, dkey, nwc):
                for tt in range(NT):
                    s = slots[tt]
                    jk = tt % 2
                    xnb = Mb[:, jk * 1024:(jk + 1) * 1024]
                    kx = [("M", jk)]
                    TS(xnb, hb[:, s, :], rstd[:, base + tt:base + tt + 1], None, ALU.mult, None,
                       [("hb", s), ("rstd", base + tt)], kx)
                    yield

                    for hf, (pb_, pk_) in enumerate(((pX, "pX"), (pO, "pO"))):
                        def tr(e, xnb=xnb, hf=hf, pb_=pb_):
                            r = None
                            for q in range(4):
                                kc = hf * 4 + q
                                r = e.matmul(pb_[:, q * 128:(q + 1) * 128], lhsT=xnb[:, kc * 128:(kc + 1) * 128], rhs=identb[:],
                                             start=True, stop=True)
                            return r
                        S.op("pe", tr, kx + ["identb"], [pk_], dur=260.0)
                        TT(dst[:, hf * 4:hf * 4 + 4, tt * 128:(tt + 1) * 128], pb_[:, :].rearrange("p (k c) -> p k c", k=4),
                           pv[:, nwc + hf * 4:nwc + hf * 4 + 4].unsqueeze(2).to_broadcast([128, 4, 128]), ALU.mult,
                           [pk_, "pv"], [dkey])

            def proj_fm(wv, slot, j, bank, src, skey):
                def fn(e):
                    r = None
                    for kc in range(KC):
                        r = e.matmul(pM[bank][:, :], lhsT=wv[:, kc, j * 128:(j + 1) * 128], rhs=src[:, kc, :],
                                     start=(kc == 0), stop=(kc == KC - 1))
                    return r
                S.op("pe", fn, [slot, skey], [("pM", bank)], dur=1750.0)

            vtm = Mb[:, 24 * TB:28 * TB].rearrange("p (t c) -> p t c", t=NT)

            def vk(tt):
                return ("M", 12 + tt // 2)

            def hgrn2_head(h, tl, wslot, wv, jq, jf, A, akey):
                par = h % 2
                t1, t2, t3, t4 = tl
                bq = mbank()
                proj_fm(wv, wslot, jq, bq, A, akey)
                ACT(T(t1), pM[bq][:, :], AF.Silu, [("pM", bq)], TK(t1))
                bf_ = mbank()
                proj_fm(wv, wslot, jf, bf_, A, akey)
                ACT(T(t2), pM[bf_][:, :], AF.Tanh, [("pM", bf_)], TK(t2), scale=0.5)
                yield
                ACT(T(t3), T(t2), AF.Ln, TK(t2) + ("dv_ab",), TK(t3), scale=dcolv(V_B + h), bias=dcolv(V_A + h))
                TS(T(t2), T(t2), dcolv(V_NB + h), dcolv(V_B + h), ALU.mult, ALU.add, TK(t2) + ("dv_ab",), TK(t2))
                SCAN(T(t4), scanmb[:, :], T(t3), 0.0, TK(t3) + ("scanmb",), TK(t4))
                b3 = T(t4).rearrange("p (c j) -> p c j", j=64)
                ACT(dcol[:, par, :], T(t4)[:, 63:TB:64], AF.Exp, TK(t4), [("dcol", par)])
                TT(T(t3).rearrange("p (c j) -> p c j", j=64), b3[:, :, 63:64].to_broadcast([128, 8, 64]), b3,
                   ALU.subtract, TK(t4), TK(t3))
                yield
                ACT(T(t4), T(t3), AF.Exp, TK(t3), TK(t4), scale=-1.0)
                ACT(T(t3), T(t3), AF.Exp, TK(t3), TK(t3))
                TT(T(t1), T(t1), T(t4), ALU.mult, TK(t1) + TK(t4), TK(t1))
                COPY("act", QD[:, par, :], T(t1), TK(t1), [("QD", par)])
                TT(KD[:, par, :], T(t2), T(t3), ALU.mult, TK(t2) + TK(t3), [("KD", par)])
                q3 = T(t1).rearrange("p (c j) -> p c j", j=64)
                TT(q3, q3, dcol[:, par, :].unsqueeze(2).to_broadcast([128, 8, 64]), ALU.mult,
                   TK(t1) + (("dcol", par),), TK(t1))
                yield

                def trk(e):
                    r = None
                    for tt in range(NT):
                        r = e.matmul(pX[:, tt * 128:(tt + 1) * 128], lhsT=KD[:, par, tt * 128:(tt + 1) * 128], rhs=identb[:],
                                     start=True, stop=True)
                    return r
                S.op("pe", trk, [("KD", par), "identb"], ["pX"], dur=260.0)
                trv = pX[:, :].rearrange("p (t c) -> p t c", t=NT)
                COPY("act", kdlo[0:64, par, :, :], trv[0:64, :, :], ["pX"], [("kdlo", par)])
                COPY("act", kdhi[64:128, par, :, :], trv[64:128, :, :], ["pX"], [("kdhi", par)])
                yield
                pSC = pX[:, :].rearrange("p (t c) -> p t c", t=NT)

                def sc(e):
                    r = None
                    for tt in range(NT):
                        r = e.matmul(pSC[:, tt, :], lhsT=KD[:, par, tt * 128:(tt + 1) * 128],
                                     rhs=QD[:, par, tt * 128:(tt + 1) * 128], start=True, stop=True)
                    return r
                S.op("pe", sc, [("KD", par), ("QD", par)], ["pX"], dur=300.0)
                TT(scm[:, par, :, :], pSC, trimb[:, :].unsqueeze(1).to_broadcast([128, NT, 128]), ALU.mult,
                   ["pX", "trimb"], [("scm", par)])
                yield

            def hgrn2_chain(h, tl):
                par = h % 2
                t1, t2, t3, t4 = tl
                pSC = pX[:, :].rearrange("p (t c) -> p t c", t=NT)

                def state(c):
                    return (Sm[:, h, :], ("Sm", h)) if c == 0 else (Sring[:, c - 1, :], ("Sring", c - 1))

                for half4 in range(2):
                    cs = range(half4 * 4, half4 * 4 + 4)

                    def pmm(e, cs=cs):
                        r = None
                        for c in cs:
                            tt, half = c // 2, c % 2
                            kd = kdlo if half == 0 else kdhi
                            r = e.matmul(pSC[:, c % 4, :], lhsT=kd[:, par, tt, :], rhs=vtm[:, tt, h * 128:(h + 1) * 128],
                                         start=True, stop=True)
                        return r
                    S.op("pe", pmm, [("kdlo", par), ("kdhi", par), ("M", 12), ("M", 13)], ["pX"], dur=300.0)
                    for c in cs:
                        sin, sink = state(c)
                        sout, soutk = (Sm[:, h, :], ("Sm", h)) if c == 7 else (Sring[:, c, :], ("Sring", c))
                        STT(sout, sin, dcol[:, par, c:c + 1], pSC[:, c % 4, :], ALU.mult, ALU.add,
                            [sink, ("dcol", par), "pX"], [soutk])
                    yield

                    def om(e, cs=cs):
                        r = None
                        for c in cs:
                            tt, half = c // 2, c % 2
                            e.matmul(pO[:, c * 64:(c + 1) * 64], lhsT=vtm[:, tt, h * 128:(h + 1) * 128],
                                     rhs=scm[:, par, tt, half * 64:(half + 1) * 64], start=True, stop=False)
                            r = e.matmul(pO[:, c * 64:(c + 1) * 64], lhsT=state(c)[0],
                                         rhs=T(t1)[:, c * 64:(c + 1) * 64], start=False, stop=True)
                        return r
                    S.op("pe", om, [("M", 12), ("M", 13), ("scm", par)] + list(TK(t1)) + [state(c)[1] for c in cs], ["pO"], dur=1300.0)
                yield
                ACT(T(t1), pO[:, :], AF.Square, ["pO"], TK(t1))
                yield
                bs = mbank()
                S.op("pe", lambda e: e.matmul(pM[bs][:, :], lhsT=onesf[:, :], rhs=T(t1), start=True, stop=True),
                     ["onesf"] + list(TK(t1)), [("pM", bs)], dur=900.0)
                ACT(T(t2), pM[bs][:, :], AF.Ln, [("pM", bs), "dv_c"], TK(t2), bias=dcolv(V_EPS))
                ACT(T(t2), T(t2), AF.Exp, TK(t2), TK(t2), scale=-0.5)
                TT(T(t3), pO[:, :], T(t2), ALU.mult, ["pO"] + list(TK(t2)), TK(t3))
                STT(actB[:, h, :], T(t3), col(C_HGNW), sgs[:, h, :], ALU.mult, ALU.mult,
                    TK(t3) + ("pv", ("sgs", h)), [("actB", h)])
                yield

            def lru_tile(t, tl, xslot, xv, A, akey):
                par = 0
                txc, tr_, ti, ta = tl
                bx_ = mbank()
                proj_fm(xv, xslot, t, bx_, A, akey)
                COPY("dve", xbuf[:, par, 0:3], halo[:, t, :], ["halo"], [("xbuf", par)])
                COPY("act", xbuf[:, par, 3:TB + 3], pM[bx_][:, :], [("pM", bx_)], [("xbuf", par)])
                COPY("dve", halo[:, t, :], xbuf[:, par, TB:TB + 3], [("xbuf", par)], ["halo"])
                TS(T(txc), xbuf[:, par, 3:TB + 3], col(C_CW + t * 4 + 3), col(C_CB + t), ALU.mult, ALU.add,
                   [("xbuf", par), "pv"], TK(txc))
                for tap in range(3):
                    STT(T(txc), xbuf[:, par, tap:tap + TB], col(C_CW + t * 4 + tap), T(txc), ALU.mult, ALU.add,
                        [("xbuf", par), "pv"] + list(TK(txc)), TK(txc))
                COPY("act", xcb[:, par, :], T(txc), TK(txc), [("xcb", par)])
                yield
                br = mbank()
                S.op("pe", lambda e: e.matmul(pM[br][:, :], lhsT=wab[:, t, :], rhs=xcb[:, par, :], start=True, stop=True),
                     ["wab", ("xcb", par)], [("pM", br)], dur=260.0)
                ACT(T(tr_), pM[br][:, :], AF.Tanh, [("pM", br), "dv_lru"], TK(tr_), scale=0.5, bias=dcolv(V_HBA + t))
                bi = mbank()
                S.op("pe", lambda e: e.matmul(pM[bi][:, :], lhsT=wxb[:, t, :], rhs=xcb[:, par, :], start=True, stop=True),
                     ["wxb", ("xcb", par)], [("pM", bi)], dur=260.0)
                ACT(T(ti), pM[bi][:, :], AF.Tanh, [("pM", bi), "dv_lru"], TK(ti), scale=0.5, bias=dcolv(V_HBX + t))
                STT(T(ti), T(ti), 1.0, T(txc), ALU.add, ALU.mult, TK(ti) + TK(txc), TK(ti))
                yield
                ACT(T(ta), T(tr_), AF.Exp, TK(tr_) + ("dv_lru",), TK(ta), scale=dcolv(V_HC + t), bias=dcolv(V_HC + t))
                ACT(T(tr_), T(tr_), AF.Exp, TK(tr_) + ("dv_lru",), TK(tr_), scale=dcolv(V_C + t), bias=dcolv(V_C + t))
                ACT(T(tr_), T(tr_), AF.Ln, TK(tr_) + ("dv_c",), TK(tr_), scale=-1.0, bias=dcolv(V_ONE))
                ACT(T(tr_), T(tr_), AF.Exp, TK(tr_) + ("dv_c",), TK(tr_), scale=0.5, bias=dcolv(V_LN05))
                TT(T(ti), T(ti), T(tr_), ALU.mult, TK(ti) + TK(tr_), TK(ti))
                SCAN(T(tr_), T(ta), T(ti), hcar[:, t:t + 1], TK(ta) + TK(ti) + ("hcar",), TK(tr_))
                COPY("dve", hcar[:, t:t + 1], T(tr_)[:, TB - 1:TB], TK(tr_), ["hcar"])
                yield

            def lru_gate(t, tl, gslot, gv, A, akey):
                txc, tr_, ti, ta = tl
                bgt = mbank()
                proj_fm(gv, gslot, t, bgt, A, akey)
                ACT(T(txc), pM[bgt][:, :], AF.Square, [("pM", bgt)], TK(txc))
                TS(T(txc), T(txc), 0.044715, 1.0, ALU.mult, ALU.add, TK(txc), TK(txc))
                TT(T(txc), pM[bgt][:, :], T(txc), ALU.mult, [("pM", bgt)] + list(TK(txc)), TK(txc))
                ACT(T(txc), T(txc), AF.Tanh, TK(txc), TK(txc), scale=0.7978845608028654)
                STT(T(txc), T(txc), 1.0, pM[bgt][:, :], ALU.add, ALU.mult, TK(txc) + (("pM", bgt),), TK(txc))
                STT(actB[:, 4 + t, :], T(tr_), 0.5, T(txc), ALU.mult, ALU.mult, TK(tr_) + TK(txc), [("actB", 4 + t)])
                yield

            def zipgen(*gens):
                gens = list(gens)
                while gens:
                    for g in list(gens):
                        try:
                            next(g)
                            yield
                        except StopIteration:
                            gens.remove(g)

            def mixer(b):
                pb = b % 2
                A = actA[:, pb, :, :]
                akey = ("actA", pb)
                slots = [pb * 4 + tt for tt in range(NT)]
                if b > 0:
                    load_x(b)
                rms_stats(slots, 0, (0, 1))
                yield
                yield from norm_transpose(slots, 0, A, akey, C_NW1)
                yield
                vs, vv = wget("s_in", 0)
                for tt in range(NT):
                    bank = mbank()

                    def fn(e, bank=bank, tt=tt):
                        r = None
                        for kc in range(KC):
                            r = e.matmul(pM[bank][:, :], lhsT=A[:, kc, tt * 128:(tt + 1) * 128], rhs=vv[:, kc, :],
                                         start=(kc == 0), stop=(kc == KC - 1))
                        return r
                    S.op("pe", fn, [vs, akey], [("pM", bank)], dur=1750.0)
                    COPY("act", vtm[:, tt, :], pM[bank][:, :], [("pM", bank)], [vk(tt)])
                    if tt % 2 == 1:
                        yield
                gs, gv = wget("s_in", 3)
                for h in range(4):
                    bg = mbank()
                    proj_fm(gv, gs, h, bg, A, akey)
                    ACT(sgs[:, h, :], pM[bg][:, :], AF.Silu, [("pM", bg)], [("sgs", h)])
                    if h % 2 == 1:
                        yield
                def hg_all():
                    for hp in range(2):
                        ws, wv = wget("s_in", 1 + hp)
                        h0, h1 = 2 * hp, 2 * hp + 1
                        yield from zipgen(hgrn2_head(h0, (0, 1, 2, 3), ws, wv, 0, 1, A, akey),
                                          hgrn2_head(h1, (4, 5, 6, 7), ws, wv, 2, 3, A, akey))
                        yield from hgrn2_chain(h0, (0, 1, 2, 3))
                        yield from hgrn2_chain(h1, (4, 5, 6, 7))

                def lru_all():
                    xs, xv = wget("s_in", 4)
                    gts, gtv = wget("s_in", 5)
                    for t in range(4):
                        yield from lru_tile(t, (8, 9, 10, 11), xs, xv, A, akey)
                        yield from lru_gate(t, (8, 9, 10, 11), gts, gtv, A, akey)
                yield from zipgen(hg_all(), lru_all())
                akeys = [("actB", i) for i in range(8)]
                for tt in range(NT):
                    s = slots[tt]
                    for dh in range(2):
                        bank = mbank()

                        def fn(e, bank=bank, tt=tt, dh=dh):
                            r = None
                            for ec in range(KC):
                                r = e.matmul(pM[bank][:, :], lhsT=actB[:, ec, tt * 128:(tt + 1) * 128],
                                             rhs=wout[:, ec, dh * 512:(dh + 1) * 512], start=(ec == 0), stop=(ec == KC - 1))
                            return r
                        S.op("pe", fn, akeys + ["wout"], [("pM", bank)], dur=1750.0)
                        TT(hb[:, s, dh * 512:(dh + 1) * 512], pM[bank][:, :], hb[:, s, dh * 512:(dh + 1) * 512], ALU.add,
                           [("pM", bank), ("hb", s)], [("hb", s)])
                    yield
                rms_stats(slots, 4, (0, 1))
                yield
                yield from norm_transpose(slots, 4, A, akey, C_NW2)

            def ffn(b):
                pb = b % 2
                A = actA[:, pb, :, :]
                akey = ("actA", pb)
                slots = [pb * 4 + tt for tt in range(NT)]
                for g in range(NG):
                    sl, rs = wget("s_gu", g)
                    for j in range(2):
                        f = 2 * g + j
                        bgate = (st8["f"] % 2) * 2
                        bup = bgate + 1
                        st8["f"] += 1

                        def mmg(e, rs=rs, j=j, bgate=bgate):
                            r = None
                            for kc in range(KC):
                                r = e.matmul(pF[bgate][:, :], lhsT=rs[:, kc, j * 128:(j + 1) * 128], rhs=A[:, kc, :],
                                             start=(kc == 0), stop=(kc == KC - 1))
                            return r

                        def mmu(e, rs=rs, j=j, bup=bup):
                            r = None
                            for kc in range(KC):
                                r = e.matmul(pF[bup][:, :], lhsT=rs[:, kc, 256 + j * 128:256 + (j + 1) * 128], rhs=A[:, kc, :],
                                             start=(kc == 0), stop=(kc == KC - 1))
                            return r
                        S.op("pe", mmg, [sl, akey], [("pF", bgate)], dur=1725.0)
                        S.op("pe", mmu, [sl, akey], [("pF", bup)], dur=1725.0)
                        sp_ = f % 2
                        ACT(sgt[:, sp_, :], pF[bgate][:, :], AF.Silu, [("pF", bgate)], [("sgt", sp_)])
                        TT(Ub[:, f * TB:(f + 1) * TB], pF[bup][:, :], sgt[:, sp_, :], ALU.mult,
                           [("pF", bup), ("sgt", sp_)], [("U", f)])
                        yield
                for dh in range(2):
                    for fg in range(3):
                        f0, nf = FG[fg]
                        sl, rs = wget("s_dn", dh * 3 + fg)

                        fis = range(2, 8) if fg == 2 else range(8)
                        for fi in fis:
                            f = f0 + fi

                            def dmm(e, rs=rs, f=f, fi=fi):
                                r = None
                                for tt in range(NT):
                                    r = e.matmul(pF[tt][:, :], lhsT=Ub[:, f * TB + tt * 128:f * TB + (tt + 1) * 128],
                                                 rhs=rs[:, fi, :], start=(f == 0), stop=(f == NF - 1), skip_group_check=True)
                                return r
                            S.op("pe", dmm, [sl, ("U", f)], [("pF", tt) for tt in range(NT)], dur=215.0 * 4)
                        yield
                    for tt in range(NT):
                        s = slots[tt]
                        TT(hb[:, s, dh * 512:(dh + 1) * 512], pF[tt][:, :], hb[:, s, dh * 512:(dh + 1) * 512], ALU.add,
                           [("pF", tt), ("hb", s)], [("hb", s)])
                    yield
                for tt in range(NT):
                    s = slots[tt]
                    ACT(sgt[:, 0, :].bitcast(BF16), hb[:, s, :], AF.Square, [("hb", s)], [("sgt", 0), ("ssq", 8 + tt)],
                        accum=ssq[:, 8 + tt:9 + tt])
                ks = [("ssq", 8 + tt) for tt in range(NT)]
                kr = [("rstd", 8 + tt) for tt in range(NT)]
                ACT(rstd[:, 8:12], ssq[:, 8:12], AF.Ln, ks + ["dv_c"], kr, scale=1.0 / D, bias=dcolv(V_EPS))
                ACT(rstd[:, 8:12], rstd[:, 8:12], AF.Exp, kr, kr, scale=-0.5)
                yield
                for tt in range(NT):
                    s = slots[tt]
                    r0 = b * TB + tt * 128
                    STT(hb[:, s, :], hb[:, s, :], rstd[:, 8 + tt:9 + tt], finw[:, :], ALU.mult, ALU.mult,
                        [("hb", s), ("rstd", 8 + tt), "finw"], [("hb", s)])
                    S.dma("pool", (lambda s, r0: lambda e: e.dma_start(out=out_d[r0:r0 + 128, :], in_=hb[:, s, :]))(s, r0),
                          ("ost", s), reads=[("hb", s)], writes=[("hb", s), ("OUT", s)])
                    yield

            for _ in mixer(0):
                st8["nm"] += 1
            for b in range(nblocks):
                gm = mixer(b + 1) if b + 1 < nblocks else None
                acc = 0.0
                for _ in ffn(b):
                    if b == 0:
                        st8["nf"] += 1
                    if gm is not None:
                        acc += ctx["ratio"]
                        while acc >= 1.0 and gm is not None:
                            acc -= 1.0
                            try:
                                next(gm)
                            except StopIteration:
                                gm = None
                if gm is not None:
                    for _ in gm:
                        pass
            S.final_wait("sp", [("OUT", s) for s in range(8)])
            return st8

        ctx["ratio"] = 1.0
        S0 = Sched(nc)
        S0.cut = True
        c0 = emit_all(S0, WStream(3, 2), WStream(3, 0))
        ctx["ratio"] = c0["nm"] / (0.75 * c0["nf"])
        print("segments", c0["nm"], c0["nf"], ctx["ratio"])
        S1 = Sched(nc)
        S1.cut = True
        WF1, WM1 = WStream(3, 2), WStream(3, 0)
        emit_all(S1, WF1, WM1)
        S = Sched(nc)
        WF, WM = WStream(3, 2, WF1.plan), WStream(3, 0, WM1.plan)
        emit_all(S, WF, WM)
        assert WF.i == len(WF1.plan) and WM.i == len(WM1.plan)
        S.emit()
    return nc


def _host_consts():
    ident = np.eye(128, dtype=np.float32)
    s = np.arange(128)[:, None]
    t = np.arange(128)[None, :]
    trimask = ((s // 64 == t // 64) & (s <= t)).astype(np.float32)
    scanmask = np.ones((1, TB), np.float32)
    scanmask[0, ::64] = 0.0
    onesm = np.full((128, 128), 1.0 / 128.0, np.float32)
    return ident, trimask, scanmask, onesm


def _pack_vec(v):
    v = np.asarray(v, np.float32).reshape(-1)
    return v.reshape(-1, 128).T


_CACHE = {}


def kernel(x, mix_norm_w, w_in, hg_lb, hg_norm_w, conv_w, conv_b, lru_wa, lru_ba,
           lru_wx, lru_bx, lru_a, w_out, ffn_norm_w, w_gate_up, w_down, final_norm_w):
    x = np.asarray(x, np.float32)
    B = x.shape[0]
    ident, trimask, scanmask, onesm = _host_consts()
    pvec = np.zeros((128, NCOL), np.float32)
    pvec[:, C_NW1:C_NW1 + 8] = _pack_vec(mix_norm_w[0])
    pvec[:, C_NW2:C_NW2 + 8] = _pack_vec(ffn_norm_w[0])
    pvec[:, C_A0:C_A0 + 4] = _pack_vec(hg_lb[0])
    pvec[:, C_A1:C_A1 + 4] = _pack_vec(hg_lb[1])
    pvec[:, C_HGNW:C_HGNW + 1] = _pack_vec(hg_norm_w[0])
    cw = np.asarray(conv_w[0], np.float32)
    for t in range(4):
        for tap in range(4):
            pvec[:, C_CW + t * 4 + tap] = cw[tap, t * 128:(t + 1) * 128]
    pvec[:, C_CB:C_CB + 4] = _pack_vec(conv_b[0])
    pvec[:, C_BA:C_BA + 4] = _pack_vec(lru_ba[0])
    pvec[:, C_BX:C_BX + 4] = _pack_vec(lru_bx[0])
    pvec[:, C_LA:C_LA + 4] = _pack_vec(lru_a[0])
    wabd = np.zeros((128, 4, 128), np.float32)
    wxbd = np.zeros((128, 4, 128), np.float32)
    wa = np.asarray(lru_wa[0], np.float32)
    wx = np.asarray(lru_wx[0], np.float32)
    for t in range(4):
        for j in range(2):
            wabd[j * 64:(j + 1) * 64, t, j * 64:(j + 1) * 64] = wa[2 * t + j]
            wxbd[j * 64:(j + 1) * 64, t, j * 64:(j + 1) * 64] = wx[2 * t + j]
    shared = {
        "w_in": np.ascontiguousarray(w_in[0], np.float32),
        "w_out": np.ascontiguousarray(w_out[0], np.float32),
        "w_gu": np.ascontiguousarray(w_gate_up[0], np.float32),
        "w_down": np.ascontiguousarray(w_down[0], np.float32),
        "pvec": pvec, "wabd": wabd, "wxbd": wxbd,
        "finw": np.asarray(final_norm_w, np.float32).reshape(1, D),
        "ident": ident, "trimask": trimask, "scanmask": scanmask, "onesm": onesm,
    }
    if "nc" not in _CACHE:
        _CACHE["nc"] = build_program()
    nc = _CACHE["nc"]
    in_maps = []
    for c in range(B):
        m = dict(shared)
        m["x"] = np.ascontiguousarray(x[c])
        in_maps.append(m)
    res = run_bass_kernel_spmd(nc, in_maps, core_ids=list(range(B)))
    return np.stack([np.asarray(r["out"], np.float32) for r in res.results], axis=0)
```

```python
import math
import numpy as np
from contextlib import ExitStack
import concourse.bass as bass
import concourse.mybir as mybir
from concourse.bass_utils import run_bass_kernel_spmd

F32 = mybir.dt.float32
BF16 = mybir.dt.bfloat16
AF = mybir.ActivationFunctionType
ALU = mybir.AluOpType

ENGS = ("pe", "act", "dve", "pool", "sp")

D = 1024
SEQ = 4096
TB = 512
NB = SEQ // TB
NT = TB // 128
KC = D // 128
DIN = 3072
DFF = 2816
NF = DFF // 128
NG = NF // 2
EPS = 1e-6
NCOL = 57
C_NW1, C_NW2, C_A0, C_A1, C_HGNW, C_CW, C_CB, C_BA, C_BX, C_LA = 0, 8, 16, 20, 24, 25, 41, 45, 49, 53
V_A, V_B, V_NB, V_HBA, V_HBX, V_C, V_HC, V_EPS, V_ONE, V_LN05, V_TMP = 0, 4, 8, 12, 16, 20, 24, 28, 29, 30, 32
NDV = 48


class Sched:
    XLAT = 120.0
    TBL = 1400.0
    WIN = 150.0
    COAL = 16
    COAL_MARGIN = 300.0

    def __init__(self, nc):
        self.nc = nc
        self.ops = []
        self.lw = {}
        self.rd = {}
        self.cut = False

    def stage(self, k):
        pass

    @staticmethod
    def _is_ps(k):
        n = k[0] if isinstance(k, tuple) else k
        return n in ("pF", "pM", "pO", "pX")

    def _add(self, eng, fn, reads, writes, dur, dsem=None, n=0, aset=None, lat=0.0):
        if self.cut:
            return
        writes = tuple(writes) + tuple(k for k in reads if self._is_ps(k))
        reads = tuple(k for k in reads if not self._is_ps(k))
        idx = len(self.ops)
        preds = set()
        for r in reads:
            p = self.lw.get(r)
            if p is not None:
                preds.add(p)
        for w in writes:
            p = self.lw.get(w)
            if p is not None:
                preds.add(p)
            preds.update(self.rd.get(w, ()))
        preds.discard(idx)
        for r in reads:
            self.rd.setdefault(r, []).append(idx)
        for w in writes:
            self.lw[w] = idx
            self.rd[w] = []
        import sys as _s
        f = _s._getframe(2)
        lines = []
        while f is not None and len(lines) < 3:
            lines.append(f.f_lineno)
            f = f.f_back
        self.ops.append(dict(eng=eng, fn=fn, preds=sorted(preds), dur=float(dur), dsem=dsem, n=n, aset=aset, lat=float(lat), lines=lines))

    def op(self, eng, fn, reads=(), writes=(), dur=500.0, aset=None):
        self._add(eng, fn, reads, writes, dur, aset=aset)

    def dma(self, eng, fn, sem, n=1, reads=(), writes=(), nbytes=1 << 19):
        self._add(eng, fn, reads, writes, 60.0 * n, dsem=("dma", sem), n=n, lat=2000.0 + nbytes / 150.0)

    def final_wait(self, eng, keys):
        self._add(eng, None, (), tuple(keys), 10.0)

    def schedule(self):
        ops = self.ops
        n = len(ops)
        succs = [[] for _ in range(n)]
        indeg = [0] * n
        for i, o in enumerate(ops):
            indeg[i] = len(o["preds"])
            for p in o["preds"]:
                succs[p].append(i)
        tail = [0.0] * n
        for i in range(n - 1, -1, -1):
            t = 0.0
            for s_ in succs[i]:
                if tail[s_] > t:
                    t = tail[s_]
            tail[i] = t + ops[i]["dur"] + ops[i]["lat"]
        ready = {e: [] for e in ENGS}
        rtime = [0.0] * n
        fin = [0.0] * n
        for i in range(n):
            if indeg[i] == 0:
                ready[ops[i]["eng"]].append(i)
        free = {e: 0.0 for e in ENGS}
        cur_set = [None]
        order = {e: [] for e in ENGS}
        done = 0
        while done < n:
            best = None
            for e in ENGS:
                rl = ready[e]
                if not rl:
                    continue
                st_min = None
                for i in rl:
                    st = max(free[e], rtime[i])
                    if e == "act" and ops[i]["aset"] is not None and cur_set[0] is not None and ops[i]["aset"] != cur_set[0]:
                        st += self.TBL
                    if st_min is None or st < st_min:
                        st_min = st
                cand = None
                for i in rl:
                    st = max(free[e], rtime[i])
                    if e == "act" and ops[i]["aset"] is not None and cur_set[0] is not None and ops[i]["aset"] != cur_set[0]:
                        st += self.TBL
                    if st <= st_min + self.WIN:
                        key = (-tail[i], i)
                        if cand is None or key < cand[0]:
                            cand = (key, i, st)
                if best is None or cand[2] < best[2]:
                    best = (e, cand[1], cand[2])
            e, i, st = best
            o = ops[i]
            ready[e].remove(i)
            if e == "act" and o["aset"] is not None:
                cur_set[0] = o["aset"]
            o["bind"] = ("eng", order[e][-1] if order[e] else None) if free[e] >= rtime[i] else ("dep", max(o["preds"], key=lambda p: fin[p]) if o["preds"] else None)
            o["start"] = st
            free[e] = st + o["dur"]
            fin[i] = st + o["dur"] + o["lat"]
            order[e].append(i)
            done += 1
            for s_ in succs[i]:
                t = fin[i] + (self.XLAT if ops[s_]["eng"] != e else 0.0)
                if t > rtime[s_]:
                    rtime[s_] = t
                indeg[s_] -= 1
                if indeg[s_] == 0:
                    ready[ops[s_]["eng"]].append(s_)
        self.order = order
        self.model_time = max(fin) if n else 0.0
        return order

    def emit(self):
        nc = self.nc
        ops = self.ops
        order = self.schedule()
        tok = [None] * len(ops)
        dcount = {}
        for e in ENGS:
            c = 0
            for i in order[e]:
                o = ops[i]
                if o["fn"] is None:
                    continue
                if o["dsem"] is None:
                    c += 1
                    tok[i] = (e, c)
                else:
                    dcount[o["dsem"]] = dcount.get(o["dsem"], 0) + o["n"]
                    tok[i] = (o["dsem"], 16 * dcount[o["dsem"]])
        with ExitStack() as st:
            sems = {}
            for e in ENGS:
                sems[e] = st.enter_context(nc.semaphore("s_" + e))
            for k, sk in enumerate(dcount):
                sems[sk] = st.enter_context(nc.semaphore("d%d" % k))
            block = st.enter_context(nc.Block())

            fin_of = {}
            for i, o in enumerate(ops):
                if tok[i] is not None:
                    fin_of[tok[i]] = o["start"] + o["dur"] + o["lat"]

            def needs_of(e_name, i):
                need = {}
                for p in ops[i]["preds"]:
                    t = tok[p]
                    if t is None:
                        continue
                    sk, v = t
                    if sk == e_name and e_name == "pe":
                        continue
                    if need.get(sk, 0) < v:
                        need[sk] = v
                return need

            def run(e_name, e):
                waited = {}
                seq = order[e_name]
                needs = [needs_of(e_name, i) for i in seq]
                for j, i in enumerate(seq):
                    o = ops[i]
                    for sk, v in needs[j].items():
                        if waited.get(sk, 0) >= v:
                            continue
                        for jj in range(j + 1, min(j + 1 + self.COAL, len(seq))):
                            v2 = needs[jj].get(sk, 0)
                            if v2 > v and fin_of.get((sk, v2), 1e30) + self.COAL_MARGIN <= o["start"]:
                                v = v2
                        waited[sk] = v
                        e.wait_ge(sems[sk], v)
                    if o["fn"] is None:
                        continue
                    r = o["fn"](e)
                    if o["dsem"] is None:
                        if isinstance(r, (list, tuple)):
                            r = r[-1]
                        r.then_inc(sems[e_name], 1)
                    else:
                        if not isinstance(r, (list, tuple)):
                            r = [r]
                        assert len(r) == o["n"], (len(r), o["n"])
                        for ins in r:
                            ins.then_inc(sems[o["dsem"]], 16)
                self.nwaits = getattr(self, "nwaits", {})
                self.nwaits[e_name] = sum(1 for _ in ())

            @block.tensor
            def _(e):
                run("pe", e)

            @block.scalar
            def _(e):
                run("act", e)

            @block.vector
            def _(e):
                run("dve", e)

            @block.gpsimd
            def _(e):
                run("pool", e)

            @block.sync
            def _(e):
                run("sp", e)


class WStream:
    def __init__(self, ns, la, plan=None):
        self.ns = ns
        self.la = la
        self.dry = plan is None
        self.plan = [] if plan is None else plan
        self.i = 0
        self.emitted = 0

    def get(self, spec, emit_load):
        i = self.i
        self.i += 1
        if self.dry:
            self.plan.append(spec)
            return i % self.ns
        assert self.plan[i] == spec, (i, self.plan[i], spec)
        hi = min(i + self.la, len(self.plan) - 1)
        while self.emitted <= hi:
            emit_load(self.plan[self.emitted], self.emitted % self.ns)
            self.emitted += 1
        return i % self.ns


def build_program(nblocks=NB):
    nc = bass.Bass("TRN2", target_bir_lowering=False)

    def din(name, shape, dt=F32):
        return nc.dram_tensor(name, shape, dt, kind="ExternalInput").ap()

    x_d = din("x", [SEQ, D])
    win_d = din("w_in", [D, DIN])
    wout_d = din("w_out", [D, D])
    wgu_d = din("w_gu", [D, 2 * DFF])
    wdn_d = din("w_down", [DFF, D])
    pvec_d = din("pvec", [128, NCOL])
    wabd_d = din("wabd", [128, 4, 128])
    wxbd_d = din("wxbd", [128, 4, 128])
    finw_d = din("finw", [1, D])
    ident_d = din("ident", [128, 128])
    trim_d = din("trimask", [128, 128])
    scanm_d = din("scanmask", [1, TB])
    ones_d = din("onesm", [128, 128])
    out_d = nc.dram_tensor("out", [SEQ, D], F32, kind="ExternalOutput").ap()
    scr = {
        "s_in": nc.dram_tensor("s_in", [6, 128, 4096], BF16, kind="Internal").ap(),
        "s_gu": nc.dram_tensor("s_gu", [NG, 128, 4096], BF16, kind="Internal").ap(),
        "s_dn": nc.dram_tensor("s_dn", [6, 128, 4096], BF16, kind="Internal").ap(),
    }

    with ExitStack() as st:
        def sb(name, shape, dt=F32):
            return st.enter_context(nc.sbuf_tensor(name, shape, dt))

        def ps(name, shape, dt=F32):
            return st.enter_context(nc.psum_tensor(name, shape, dt))

        wrF = sb("wrF", [128, 3, 4096], BF16)
        wrM = sb("wrM", [128, 3, 4096], BF16)
        wout = sb("wout", [128, KC, D], BF16)
        hb = sb("hb", [128, 8, D], F32)
        actA = sb("actA", [128, 2, KC, TB], BF16)
        actB = sb("actB", [128, KC, TB], BF16)
        U = sb("U", [128, NF * TB // 2], F32)
        Ub = U.bitcast(BF16)
        M = sb("M", [128, 14 * TB], F32)
        Mb = M.bitcast(BF16)
        sgt = sb("sgt", [128, 2, TB], F32)
        kdlo = sb("kdlo", [128, 2, NT, 128], BF16)
        kdhi = sb("kdhi", [128, 2, NT, 128], BF16)
        QD = sb("QD", [128, 2, TB], BF16)
        KD = sb("KD", [128, 2, TB], BF16)
        scm = sb("scm", [128, 2, NT, 128], BF16)
        sgs = sb("sgs", [128, 4, TB], BF16)
        Sring = sb("Sring", [128, 8, 128], F32)
        Sm = sb("Sm", [128, 4, 128], F32)
        dcol = sb("dcol", [128, 2, 8], F32)
        xbuf = sb("xbuf", [128, 1, TB + 3], F32)
        halo = sb("halo", [128, 4, 3], F32)
        hcar = sb("hcar", [128, 4], F32)
        xcb = sb("xcb", [128, 1, TB], BF16)
        identb = sb("identb", [128, 128], BF16)
        trimb = sb("trimb", [128, 128], BF16)
        scanmb = sb("scanmb", [128, TB], BF16)
        onesf = sb("onesf", [128, 128], F32)
        finw = sb("finw_s", [128, D], F32)
        wab = sb("wab", [128, 4, 128], BF16)
        wxb = sb("wxb", [128, 4, 128], BF16)
        pv = sb("pv", [128, NCOL], F32)
        dv = sb("dv", [128, NDV], F32)
        ssq = sb("ssq", [128, 16], F32)
        rstd = sb("rstd", [128, 16], F32)

        pF = [ps("pf%d" % i, [128, TB], F32) for i in range(4)]
        pM = [ps("pm%d" % i, [128, TB], F32) for i in range(2)]
        pO = ps("po", [128, TB], F32)
        pX = ps("px", [128, TB], F32)
        pXb = pX.bitcast(BF16)

        ctx = {}

        def emit_all(S, WF, WM):
            st8 = {"m": 0, "f": 0, "set": "S", "nm": 0, "nf": 0}

            def T(i):
                return M[:, i * TB:(i + 1) * TB]

            def TK(i):
                return (("M", i),)

            def ACT(out, in_, func, reads, writes, scale=1.0, bias=None, accum=None):
                if func in (AF.Silu, AF.Tanh):
                    st8["set"] = "S"
                elif func in (AF.Ln, AF.Exp):
                    st8["set"] = "L"

                def fn(e):
                    kw = dict(out=out, in_=in_, func=func, scale=scale)
                    if bias is not None:
                        kw["bias"] = bias
                    if accum is not None:
                        kw["accum_out"] = accum
                    return e.activation(**kw)
                aset = "S" if func in (AF.Silu, AF.Tanh) else ("L" if func in (AF.Ln, AF.Exp) else None)
                S.op("act", fn, reads, writes, dur=220.0 + 0.833 * in_.free_size() + (100.0 if accum is not None else 0.0), aset=aset)

            def TS(out, in0, s1, s2, op0, op1, reads, writes, eng="dve"):
                def fn(e):
                    if s2 is None:
                        return e.tensor_scalar(out=out, in0=in0, scalar1=s1, scalar2=None, op0=op0)
                    return e.tensor_scalar(out=out, in0=in0, scalar1=s1, scalar2=s2, op0=op0, op1=op1)
                S.op(eng, fn, reads, writes, dur=150.0 + 0.7 * out.free_size())

            def TT(out, in0, in1, op, reads, writes, eng="dve"):
                S.op(eng, lambda e: e.tensor_tensor(out=out, in0=in0, in1=in1, op=op), reads, writes, dur=150.0 + 1.04 * out.free_size())

            def STT(out, in0, scalar, in1, op0, op1, reads, writes):
                S.op("dve", lambda e: e.scalar_tensor_tensor(out=out, in0=in0, scalar=scalar, in1=in1, op0=op0, op1=op1), reads, writes, dur=150.0 + 1.04 * out.free_size())

            def SCAN(out, d0, d1, init, reads, writes):
                S.op("dve", lambda e: e.tensor_tensor_scan(out=out, data0=d0, data1=d1, initial=init, op0=ALU.mult, op1=ALU.add), reads, writes, dur=150.0 + 2.1 * out.free_size())

            def COPY(eng, out, in_, reads, writes):
                if eng == "act":
                    ACT(out, in_, AF.Copy, reads, writes)
                else:
                    S.op(eng, lambda e: e.tensor_copy(out=out, in_=in_), reads, writes, dur=150.0 + 0.6 * out.free_size())

            def MEMSET(eng, ap, val, writes):
                S.op(eng, lambda e: e.memset(ap, val), (), writes, dur=200.0)

            def mbank():
                i = st8["m"] % 2
                st8["m"] += 1
                return i

            def col(c):
                return pv[:, c:c + 1]

            def dcolv(c):
                return dv[:, c:c + 1]

            def mk_load(ring, rk):
                def emit_load(spec, slot):
                    name, idx = spec
                    S.dma("sp", lambda e: e.dma_start(out=ring[:, slot, :], in_=scr[name][idx]), (rk, slot),
                          reads=[(name, idx)], writes=[(rk, slot)], nbytes=1 << 20)
                return emit_load
            loadF = mk_load(wrF, "wrF")
            loadM = mk_load(wrM, "wrM")

            def wget(name, idx):
                if name == "s_in":
                    slot = WM.get((name, idx), loadM)
                    return ("wrM", slot), wrM[:, slot, :].rearrange("p (k c) -> p k c", k=8)
                slot = WF.get((name, idx), loadF)
                return ("wrF", slot), wrF[:, slot, :].rearrange("p (k c) -> p k c", k=8)

            def load_x(b):
                for tt in range(NT):
                    s = (b % 2) * 4 + tt
                    r0 = b * TB + tt * 128
                    S.dma("pool", (lambda s, r0: lambda e: e.dma_start(out=hb[:, s, :], in_=x_d[r0:r0 + 128, :]))(s, r0),
                          ("hb", s), writes=[("hb", s)])

            S.dma("sp", lambda e: e.dma_start(out=pv[:], in_=pvec_d), "pv", writes=["pv"])
            S.dma("sp", lambda e: e.dma_start(out=onesf[:], in_=ones_d), "onesf", writes=["onesf"])
            S.dma("sp", lambda e: e.dma_start(out=finw[:], in_=finw_d.partition_broadcast(128)), "finw", writes=["finw"])
            load_x(0)
            S.dma("pool", lambda e: e.dma_start(out=identb[:], in_=ident_d), "identb", writes=["identb"])
            S.dma("pool", lambda e: e.dma_start(out=trimb[:], in_=trim_d), "trimb", writes=["trimb"])
            S.dma("pool", lambda e: e.dma_start(out=scanmb[:], in_=scanm_d.partition_broadcast(128)), "scanmb", writes=["scanmb"])
            S.dma("pool", lambda e: e.dma_start(out=wab[:], in_=wabd_d), "wab", writes=["wab"])
            S.dma("pool", lambda e: e.dma_start(out=wxb[:], in_=wxbd_d), "wxb", writes=["wxb"])

            def img(name, idx):
                return scr[name][idx].rearrange("p (k c) -> p k c", k=8)

            cast_hist = []

            def cast_dep():
                return [cast_hist[-3]] if len(cast_hist) >= 3 else []

            def cast_in(idx, pieces):
                def fn(e):
                    r = []
                    for j, (c0, w) in enumerate(pieces):
                        dst = img("s_in", idx)[:, :, j * w:(j + 1) * w]
                        r.append(e.dma_start(out=dst, in_=win_d[:, c0:c0 + w].rearrange("(k p) c -> p k c", p=128)))
                    return r
                S.dma("pool", fn, ("s_in", idx), n=len(pieces), reads=cast_dep(), writes=[("s_in", idx)], nbytes=3 << 20)
                cast_hist.append(("s_in", idx))

            cast_in(0, [(1024, 512)])
            cast_in(3, [(1536, 512)])
            cast_in(1, [(0, 128), (512, 128), (128, 128), (640, 128)])
            cast_in(2, [(256, 128), (768, 128), (384, 128), (896, 128)])
            cast_in(4, [(2048, 512)])
            cast_in(5, [(2560, 512)])
            S.dma("pool", lambda e: e.dma_start(out=wout[:], in_=wout_d.rearrange("(e p) d -> p e d", p=128)), "wout",
                  reads=cast_dep(), writes=["wout"], nbytes=6 << 20)
            cast_hist.append("wout")

            def cast_gu(g):
                def fn(e):
                    a = e.dma_start(out=img("s_gu", g)[:, :, 0:256], in_=wgu_d[:, g * 256:(g + 1) * 256].rearrange("(k p) c -> p k c", p=128))
                    bb = e.dma_start(out=img("s_gu", g)[:, :, 256:512], in_=wgu_d[:, DFF + g * 256:DFF + (g + 1) * 256].rearrange("(k p) c -> p k c", p=128))
                    return [a, bb]
                S.dma("pool", fn, ("s_gu", g), n=2, reads=cast_dep(), writes=[("s_gu", g)], nbytes=3 << 20)
                cast_hist.append(("s_gu", g))

            FG = [(0, 8), (8, 8), (14, 8)]

            def cast_dn(j):
                dh, fg = j // 3, j % 3
                f0, nf = FG[fg]
                S.dma("pool", lambda e: e.dma_start(out=img("s_dn", j)[:, 0:nf, :],
                                                    in_=wdn_d[f0 * 128:(f0 + nf) * 128, dh * 512:(dh + 1) * 512].rearrange("(f p) c -> p f c", p=128)),
                      ("s_dn", j), reads=cast_dep(), writes=[("s_dn", j)], nbytes=3 << 20)
                cast_hist.append(("s_dn", j))

            for g in range(NG):
                cast_gu(g)
            for j in range(6):
                cast_dn(j)

            MEMSET("dve", Sm[:], 0.0, [("Sm", h) for h in range(4)])
            MEMSET("dve", halo[:], 0.0, ["halo"])
            MEMSET("dve", hcar[:], 0.0, ["hcar"])
            MEMSET("dve", kdlo[:], 0.0, [("kdlo", 0), ("kdlo", 1)])
            MEMSET("dve", kdhi[:], 0.0, [("kdhi", 0), ("kdhi", 1)])
            MEMSET("dve", dv[:, V_EPS:V_EPS + 1], EPS, ["dv_c"])
            MEMSET("dve", dv[:, V_ONE:V_ONE + 1], 1.0, ["dv_c"])
            MEMSET("dve", dv[:, V_LN05:V_LN05 + 1], math.log(0.5), ["dv_c"])
            TT(dv[:, V_TMP:V_TMP + 4], pv[:, C_A0:C_A0 + 4], pv[:, C_A1:C_A1 + 4], ALU.subtract, ["pv"], ["dv_t"])
            ACT(dv[:, V_TMP:V_TMP + 4], dv[:, V_TMP:V_TMP + 4], AF.Tanh, ["dv_t"], ["dv_t"], scale=0.5)
            TS(dv[:, V_A:V_A + 4], dv[:, V_TMP:V_TMP + 4], 0.25, 0.75, ALU.mult, ALU.add, ["dv_t"], ["dv_ab"])
            TS(dv[:, V_B:V_B + 4], dv[:, V_TMP:V_TMP + 4], -0.25, 0.25, ALU.mult, ALU.add, ["dv_t"], ["dv_ab"])
            TS(dv[:, V_NB:V_NB + 4], dv[:, V_TMP:V_TMP + 4], 0.25, -0.25, ALU.mult, ALU.add, ["dv_t"], ["dv_ab"])
            TS(dv[:, V_HBA:V_HBA + 4], pv[:, C_BA:C_BA + 4], 0.5, None, ALU.mult, None, ["pv"], ["dv_lru"])
            TS(dv[:, V_HBX:V_HBX + 4], pv[:, C_BX:C_BX + 4], 0.5, None, ALU.mult, None, ["pv"], ["dv_lru"])
            ACT(dv[:, V_TMP + 4:V_TMP + 8], pv[:, C_LA:C_LA + 4], AF.Exp, ["pv"], ["dv_t2"], scale=-1.0)
            ACT(dv[:, V_TMP + 4:V_TMP + 8], dv[:, V_TMP + 4:V_TMP + 8], AF.Ln, ["dv_t2", "dv_c"], ["dv_t2"], bias=dcolv(V_ONE))
            TS(dv[:, V_C:V_C + 4], dv[:, V_TMP + 4:V_TMP + 8], -8.0, None, ALU.mult, None, ["dv_t2"], ["dv_lru"])
            TS(dv[:, V_HC:V_HC + 4], dv[:, V_TMP + 4:V_TMP + 8], -4.0, None, ALU.mult, None, ["dv_t2"], ["dv_lru"])

            def rms_stats(slots, base, junk):
                for tt in range(NT):
                    s = slots[tt]
                    jk = junk[tt % 2]
                    ACT(Mb[:, jk * 1024:(jk + 1) * 1024], hb[:, s, :], AF.Square, [("hb", s)],
                        [("M", jk), ("ssq", base + tt)], accum=ssq[:, base + tt:base + tt + 1])
                ks = [("ssq", base + tt) for tt in range(NT)]
                kr = [("rstd", base + tt) for tt in range(NT)]
                ACT(rstd[:, base:base + 4], ssq[:, base:base + 4], AF.Ln, ks + ["dv_c"], kr, scale=1.0 / D, bias=dcolv(V_EPS))
                ACT(rstd[:, base:base + 4], rstd[:, base:base + 4], AF.Exp, kr, kr, scale=-0.5)

            def norm_transpose(slots, base, dst, dkey, nwc):
                for tt in range(NT):
                    s = slots[tt]
                    jk = tt % 2
                    xnb = Mb[:, jk * 1024:(jk + 1) * 1024]
                    kx = [("M", jk)]
                    TS(xnb, hb[:, s, :], rstd[:, base + tt:base + tt + 1], None, ALU.mult, None,
                       [("hb", s), ("rstd", base + tt)], kx)
                    yield

                    for hf, (pb_, pk_) in enumerate(((pX, "pX"), (pO, "pO"))):
                        def tr(e, xnb=xnb, hf=hf, pb_=pb_):
                            r = None
                            for q in range(4):
                                kc = hf * 4 + q
                                r = e.matmul(pb_[:, q * 128:(q + 1) * 128], lhsT=xnb[:, kc * 128:(kc + 1) * 128], rhs=identb[:],
                                             start=True, stop=True)
                            return r
                        S.op("pe", tr, kx + ["identb"], [pk_], dur=260.0)
                        TT(dst[:, hf * 4:hf * 4 + 4, tt * 128:(tt + 1) * 128], pb_[:, :].rearrange("p (k c) -> p k c", k=4),
                           pv[:, nwc + hf * 4:nwc + hf * 4 + 4].unsqueeze(2).to_broadcast([128, 4, 128]), ALU.mult,
                           [pk_, "pv"], [dkey])

            def proj_fm(wv, slot, j, bank, src, skey):
                def fn(e):
                    r = None
                    for kc in range(KC):
                        r = e.matmul(pM[bank][:, :], lhsT=wv[:, kc, j * 128:(j + 1) * 128], rhs=src[:, kc, :],
                                     start=(kc == 0), stop=(kc == KC - 1))
                    return r
                S.op("pe", fn, [slot, skey], [("pM", bank)], dur=1750.0)

            vtm = Mb[:, 24 * TB:28 * TB].rearrange("p (t c) -> p t c", t=NT)

            def vk(tt):
                return ("M", 12 + tt // 2)

            def hgrn2_head(h, tl, wslot, wv, jq, jf, A, akey):
                par = h % 2
                t1, t2, t3, t4 = tl
                bq = mbank()
                proj_fm(wv, wslot, jq, bq, A, akey)
                ACT(T(t1), pM[bq][:, :], AF.Silu, [("pM", bq)], TK(t1))
                bf_ = mbank()
                proj_fm(wv, wslot, jf, bf_, A, akey)
                ACT(T(t2), pM[bf_][:, :], AF.Tanh, [("pM", bf_)], TK(t2), scale=0.5)
                yield
                ACT(T(t3), T(t2), AF.Ln, TK(t2) + ("dv_ab",), TK(t3), scale=dcolv(V_B + h), bias=dcolv(V_A + h))
                TS(T(t2), T(t2), dcolv(V_NB + h), dcolv(V_B + h), ALU.mult, ALU.add, TK(t2) + ("dv_ab",), TK(t2))
                SCAN(T(t4), scanmb[:, :], T(t3), 0.0, TK(t3) + ("scanmb",), TK(t4))
                b3 = T(t4).rearrange("p (c j) -> p c j", j=64)
                ACT(dcol[:, par, :], T(t4)[:, 63:TB:64], AF.Exp, TK(t4), [("dcol", par)])
                TT(T(t3).rearrange("p (c j) -> p c j", j=64), b3[:, :, 63:64].to_broadcast([128, 8, 64]), b3,
                   ALU.subtract, TK(t4), TK(t3))
                yield
                ACT(T(t4), T(t3), AF.Exp, TK(t3), TK(t4), scale=-1.0)
                ACT(T(t3), T(t3), AF.Exp, TK(t3), TK(t3))
                TT(T(t1), T(t1), T(t4), ALU.mult, TK(t1) + TK(t4), TK(t1))
                COPY("act", QD[:, par, :], T(t1), TK(t1), [("QD", par)])
                TT(KD[:, par, :], T(t2), T(t3), ALU.mult, TK(t2) + TK(t3), [("KD", par)])
                q3 = T(t1).rearrange("p (c j) -> p c j", j=64)
                TT(q3, q3, dcol[:, par, :].unsqueeze(2).to_broadcast([128, 8, 64]), ALU.mult,
                   TK(t1) + (("dcol", par),), TK(t1))
                yield

                def trk(e):
                    r = None
                    for tt in range(NT):
                        r = e.matmul(pX[:, tt * 128:(tt + 1) * 128], lhsT=KD[:, par, tt * 128:(tt + 1) * 128], rhs=identb[:],
                                     start=True, stop=True)
                    return r
                S.op("pe", trk, [("KD", par), "identb"], ["pX"], dur=260.0)
                trv = pX[:, :].rearrange("p (t c) -> p t c", t=NT)
                COPY("act", kdlo[0:64, par, :, :], trv[0:64, :, :], ["pX"], [("kdlo", par)])
                COPY("act", kdhi[64:128, par, :, :], trv[64:128, :, :], ["pX"], [("kdhi", par)])
                yield
                pSC = pX[:, :].rearrange("p (t c) -> p t c", t=NT)

                def sc(e):
                    r = None
                    for tt in range(NT):
                        r = e.matmul(pSC[:, tt, :], lhsT=KD[:, par, tt * 128:(tt + 1) * 128],
                                     rhs=QD[:, par, tt * 128:(tt + 1) * 128], start=True, stop=True)
                    return r
                S.op("pe", sc, [("KD", par), ("QD", par)], ["pX"], dur=300.0)
                TT(scm[:, par, :, :], pSC, trimb[:, :].unsqueeze(1).to_broadcast([128, NT, 128]), ALU.mult,
                   ["pX", "trimb"], [("scm", par)])
                yield

            def hgrn2_chain(h, tl):
                par = h % 2
                t1, t2, t3, t4 = tl
                pSC = pX[:, :].rearrange("p (t c) -> p t c", t=NT)

                def state(c):
                    return (Sm[:, h, :], ("Sm", h)) if c == 0 else (Sring[:, c - 1, :], ("Sring", c - 1))

                for half4 in range(2):
                    cs = range(half4 * 4, half4 * 4 + 4)

                    def pmm(e, cs=cs):
                        r = None
                        for c in cs:
                            tt, half = c // 2, c % 2
                            kd = kdlo if half == 0 else kdhi
                            r = e.matmul(pSC[:, c % 4, :], lhsT=kd[:, par, tt, :], rhs=vtm[:, tt, h * 128:(h + 1) * 128],
                                         start=True, stop=True)
                        return r
                    S.op("pe", pmm, [("kdlo", par), ("kdhi", par), ("M", 12), ("M", 13)], ["pX"], dur=300.0)
                    for c in cs:
                        sin, sink = state(c)
                        sout, soutk = (Sm[:, h, :], ("Sm", h)) if c == 7 else (Sring[:, c, :], ("Sring", c))
                        STT(sout, sin, dcol[:, par, c:c + 1], pSC[:, c % 4, :], ALU.mult, ALU.add,
                            [sink, ("dcol", par), "pX"], [soutk])
                    yield

                    def om(e, cs=cs):
                        r = None
                        for c in cs:
                            tt, half = c // 2, c % 2
                            e.matmul(pO[:, c * 64:(c + 1) * 64], lhsT=vtm[:, tt, h * 128:(h + 1) * 128],
                                     rhs=scm[:, par, tt, half * 64:(half + 1) * 64], start=True, stop=False)
                            r = e.matmul(pO[:, c * 64:(c + 1) * 64], lhsT=state(c)[0],
                                         rhs=T(t1)[:, c * 64:(c + 1) * 64], start=False, stop=True)
                        return r
                    S.op("pe", om, [("M", 12), ("M", 13), ("scm", par)] + list(TK(t1)) + [state(c)[1] for c in cs], ["pO"], dur=1300.0)
                yield
                ACT(T(t1), pO[:, :], AF.Square, ["pO"], TK(t1))
                yield
                bs = mbank()
                S.op("pe", lambda e: e.matmul(pM[bs][:, :], lhsT=onesf[:, :], rhs=T(t1), start=True, stop=True),
                     ["onesf"] + list(TK(t1)), [("pM", bs)], dur=900.0)
                ACT(T(t2), pM[bs][:, :], AF.Ln, [("pM", bs), "dv_c"], TK(t2), bias=dcolv(V_EPS))
                ACT(T(t2), T(t2), AF.Exp, TK(t2), TK(t2), scale=-0.5)
                TT(T(t3), pO[:, :], T(t2), ALU.mult, ["pO"] + list(TK(t2)), TK(t3))
                STT(actB[:, h, :], T(t3), col(C_HGNW), sgs[:, h, :], ALU.mult, ALU.mult,
                    TK(t3) + ("pv", ("sgs", h)), [("actB", h)])
                yield

            def lru_tile(t, tl, xslot, xv, A, akey):
                par = 0
                txc, tr_, ti, ta = tl
                bx_ = mbank()
                proj_fm(xv, xslot, t, bx_, A, akey)
                COPY("dve", xbuf[:, par, 0:3], halo[:, t, :], ["halo"], [("xbuf", par)])
                COPY("act", xbuf[:, par, 3:TB + 3], pM[bx_][:, :], [("pM", bx_)], [("xbuf", par)])
                COPY("dve", halo[:, t, :], xbuf[:, par, TB:TB + 3], [("xbuf", par)], ["halo"])
                TS(T(txc), xbuf[:, par, 3:TB + 3], col(C_CW + t * 4 + 3), col(C_CB + t), ALU.mult, ALU.add,
                   [("xbuf", par), "pv"], TK(txc))
                for tap in range(3):
                    STT(T(txc), xbuf[:, par, tap:tap + TB], col(C_CW + t * 4 + tap), T(txc), ALU.mult, ALU.add,
                        [("xbuf", par), "pv"] + list(TK(txc)), TK(txc))
                COPY("act", xcb[:, par, :], T(txc), TK(txc), [("xcb", par)])
                yield
                br = mbank()
                S.op("pe", lambda e: e.matmul(pM[br][:, :], lhsT=wab[:, t, :], rhs=xcb[:, par, :], start=True, stop=True),
                     ["wab", ("xcb", par)], [("pM", br)], dur=260.0)
                ACT(T(tr_), pM[br][:, :], AF.Tanh, [("pM", br), "dv_lru"], TK(tr_), scale=0.5, bias=dcolv(V_HBA + t))
                bi = mbank()
                S.op("pe", lambda e: e.matmul(pM[bi][:, :], lhsT=wxb[:, t, :], rhs=xcb[:, par, :], start=True, stop=True),
                     ["wxb", ("xcb", par)], [("pM", bi)], dur=260.0)
                ACT(T(ti), pM[bi][:, :], AF.Tanh, [("pM", bi), "dv_lru"], TK(ti), scale=0.5, bias=dcolv(V_HBX + t))
                STT(T(ti), T(ti), 1.0, T(txc), ALU.add, ALU.mult, TK(ti) + TK(txc), TK(ti))
                yield
                ACT(T(ta), T(tr_), AF.Exp, TK(tr_) + ("dv_lru",), TK(ta), scale=dcolv(V_HC + t), bias=dcolv(V_HC + t))
                ACT(T(tr_), T(tr_), AF.Exp, TK(tr_) + ("dv_lru",), TK(tr_), scale=dcolv(V_C + t), bias=dcolv(V_C + t))
                ACT(T(tr_), T(tr_), AF.Ln, TK(tr_) + ("dv_c",), TK(tr_), scale=-1.0, bias=dcolv(V_ONE))
                ACT(T(tr_), T(tr_), AF.Exp, TK(tr_) + ("dv_c",), TK(tr_), scale=0.5, bias=dcolv(V_LN05))
                TT(T(ti), T(ti), T(tr_), ALU.mult, TK(ti) + TK(tr_), TK(ti))
                SCAN(T(tr_), T(ta), T(ti), hcar[:, t:t + 1], TK(ta) + TK(ti) + ("hcar",), TK(tr_))
                COPY("dve", hcar[:, t:t + 1], T(tr_)[:, TB - 1:TB], TK(tr_), ["hcar"])
                yield

            def lru_gate(t, tl, gslot, gv, A, akey):
                txc, tr_, ti, ta = tl
                bgt = mbank()
                proj_fm(gv, gslot, t, bgt, A, akey)
                ACT(T(txc), pM[bgt][:, :], AF.Square, [("pM", bgt)], TK(txc))
                TS(T(txc), T(txc), 0.044715, 1.0, ALU.mult, ALU.add, TK(txc), TK(txc))
                TT(T(txc), pM[bgt][:, :], T(txc), ALU.mult, [("pM", bgt)] + list(TK(txc)), TK(txc))
                ACT(T(txc), T(txc), AF.Tanh, TK(txc), TK(txc), scale=0.7978845608028654)
                STT(T(txc), T(txc), 1.0, pM[bgt][:, :], ALU.add, ALU.mult, TK(txc) + (("pM", bgt),), TK(txc))
                STT(actB[:, 4 + t, :], T(tr_), 0.5, T(txc), ALU.mult, ALU.mult, TK(tr_) + TK(txc), [("actB", 4 + t)])
                yield

            def zipgen(*gens):
                gens = list(gens)
                while gens:
                    for g in list(gens):
                        try:
                            next(g)
                            yield
                        except StopIteration:
                            gens.remove(g)

            def mixer(b):
                pb = b % 2
                A = actA[:, pb, :, :]
                akey = ("actA", pb)
                slots = [pb * 4 + tt for tt in range(NT)]
                if b > 0:
                    load_x(b)
                rms_stats(slots, 0, (0, 1))
                yield
                yield from norm_transpose(slots, 0, A, akey, C_NW1)
                yield
                vs, vv = wget("s_in", 0)
                for tt in range(NT):
                    bank = mbank()

                    def fn(e, bank=bank, tt=tt):
                        r = None
                        for kc in range(KC):
                            r = e.matmul(pM[bank][:, :], lhsT=A[:, kc, tt * 128:(tt + 1) * 128], rhs=vv[:, kc, :],
                                         start=(kc == 0), stop=(kc == KC - 1))
                        return r
                    S.op("pe", fn, [vs, akey], [("pM", bank)], dur=1750.0)
                    COPY("act", vtm[:, tt, :], pM[bank][:, :], [("pM", bank)], [vk(tt)])
                    if tt % 2 == 1:
                        yield
                gs, gv = wget("s_in", 3)
                for h in range(4):
                    bg = mbank()
                    proj_fm(gv, gs, h, bg, A, akey)
                    ACT(sgs[:, h, :], pM[bg][:, :], AF.Silu, [("pM", bg)], [("sgs", h)])
                    if h % 2 == 1:
                        yield
                def hg_all():
                    for hp in range(2):
                        ws, wv = wget("s_in", 1 + hp)
                        h0, h1 = 2 * hp, 2 * hp + 1
                        yield from zipgen(hgrn2_head(h0, (0, 1, 2, 3), ws, wv, 0, 1, A, akey),
                                          hgrn2_head(h1, (4, 5, 6, 7), ws, wv, 2, 3, A, akey))
                        yield from hgrn2_chain(h0, (0, 1, 2, 3))
                        yield from hgrn2_chain(h1, (4, 5, 6, 7))

                def lru_all():
                    xs, xv = wget("s_in", 4)
                    gts, gtv = wget("s_in", 5)
                    for t in range(4):
                        yield from lru_tile(t, (8, 9, 10, 11), xs, xv, A, akey)
                        yield from lru_gate(t, (8, 9, 10, 11), gts, gtv, A, akey)
                yield from zipgen(hg_all(), lru_all())
                akeys = [("actB", i) for i in range(8)]
                for tt in range(NT):
                    s = slots[tt]
                    for dh in range(2):
                        bank = mbank()

                        def fn(e, bank=bank, tt=tt, dh=dh):
                            r = None
                            for ec in range(KC):
                                r = e.matmul(pM[bank][:, :], lhsT=actB[:, ec, tt * 128:(tt + 1) * 128],
                                             rhs=wout[:, ec, dh * 512:(dh + 1) * 512], start=(ec == 0), stop=(ec == KC - 1))
                            return r
                        S.op("pe", fn, akeys + ["wout"], [("pM", bank)], dur=1750.0)
                        TT(hb[:, s, dh * 512:(dh + 1) * 512], pM[bank][:, :], hb[:, s, dh * 512:(dh + 1) * 512], ALU.add,
                           [("pM", bank), ("hb", s)], [("hb", s)])
                    yield
                rms_stats(slots, 4, (0, 1))
                yield
                yield from norm_transpose(slots, 4, A, akey, C_NW2)

            def ffn(b):
                pb = b % 2
                A = actA[:, pb, :, :]
                akey = ("actA", pb)
                slots = [pb * 4 + tt for tt in range(NT)]
                for g in range(NG):
                    sl, rs = wget("s_gu", g)
                    for j in range(2):
                        f = 2 * g + j
                        bgate = (st8["f"] % 2) * 2
                        bup = bgate + 1
                        st8["f"] += 1

                        def mmg(e, rs=rs, j=j, bgate=bgate):
                            r = None
                            for kc in range(KC):
                                r = e.matmul(pF[bgate][:, :], lhsT=rs[:, kc, j * 128:(j + 1) * 128], rhs=A[:, kc, :],
                                             start=(kc == 0), stop=(kc == KC - 1))
                            return r

                        def mmu(e, rs=rs, j=j, bup=bup):
                            r = None
                            for kc in range(KC):
                                r = e.matmul(pF[bup][:, :], lhsT=rs[:, kc, 256 + j * 128:256 + (j + 1) * 128], rhs=A[:, kc, :],
                                             start=(kc == 0), stop=(kc == KC - 1))
                            return r
                        S.op("pe", mmg, [sl, akey], [("pF", bgate)], dur=1725.0)
                        S.op("pe", mmu, [sl, akey], [("pF", bup)], dur=1725.0)
                        sp_ = f % 2
                        ACT(sgt[:, sp_, :], pF[bgate][:, :], AF.Silu, [("pF", bgate)], [("sgt", sp_)])
                        TT(Ub[:, f * TB:(f + 1) * TB], pF[bup][:, :], sgt[:, sp_, :], ALU.mult,
                           [("pF", bup), ("sgt", sp_)], [("U", f)])
                        yield
                for dh in range(2):
                    for fg in range(3):
                        f0, nf = FG[fg]
                        sl, rs = wget("s_dn", dh * 3 + fg)

                        fis = range(2, 8) if fg == 2 else range(8)
                        for fi in fis:
                            f = f0 + fi

                            def dmm(e, rs=rs, f=f, fi=fi):
                                r = None
                                for tt in range(NT):
                                    r = e.matmul(pF[tt][:, :], lhsT=Ub[:, f * TB + tt * 128:f * TB + (tt + 1) * 128],
                                                 rhs=rs[:, fi, :], start=(f == 0), stop=(f == NF - 1), skip_group_check=True)
                                return r
                            S.op("pe", dmm, [sl, ("U", f)], [("pF", tt) for tt in range(NT)], dur=215.0 * 4)
                        yield
                    for tt in range(NT):
                        s = slots[tt]
                        TT(hb[:, s, dh * 512:(dh + 1) * 512], pF[tt][:, :], hb[:, s, dh * 512:(dh + 1) * 512], ALU.add,
                           [("pF", tt), ("hb", s)], [("hb", s)])
                    yield
                for tt in range(NT):
                    s = slots[tt]
                    ACT(sgt[:, 0, :].bitcast(BF16), hb[:, s, :], AF.Square, [("hb", s)], [("sgt", 0), ("ssq", 8 + tt)],
                        accum=ssq[:, 8 + tt:9 + tt])
                ks = [("ssq", 8 + tt) for tt in range(NT)]
                kr = [("rstd", 8 + tt) for tt in range(NT)]
                ACT(rstd[:, 8:12], ssq[:, 8:12], AF.Ln, ks + ["dv_c"], kr, scale=1.0 / D, bias=dcolv(V_EPS))
                ACT(rstd[:, 8:12], rstd[:, 8:12], AF.Exp, kr, kr, scale=-0.5)
                yield
                for tt in range(NT):
                    s = slots[tt]
                    r0 = b * TB + tt * 128
                    STT(hb[:, s, :], hb[:, s, :], rstd[:, 8 + tt:9 + tt], finw[:, :], ALU.mult, ALU.mult,
                        [("hb", s), ("rstd", 8 + tt), "finw"], [("hb", s)])
                    S.dma("pool", (lambda s, r0: lambda e: e.dma_start(out=out_d[r0:r0 + 128, :], in_=hb[:, s, :]))(s, r0),
                          ("ost", s), reads=[("hb", s)], writes=[("hb", s), ("OUT", s)])
                    yield

            for _ in mixer(0):
                st8["nm"] += 1
            for b in range(nblocks):
                gm = mixer(b + 1) if b + 1 < nblocks else None
                acc = 0.0
                for _ in ffn(b):
                    if b == 0:
                        st8["nf"] += 1
                    if gm is not None:
                        acc += ctx["ratio"]
                        while acc >= 1.0 and gm is not None:
                            acc -= 1.0
                            try:
                                next(gm)
                            except StopIteration:
                                gm = None
                if gm is not None:
                    for _ in gm:
                        pass
            S.final_wait("sp", [("OUT", s) for s in range(8)])
            return st8

        ctx["ratio"] = 1.0
        S0 = Sched(nc)
        S0.cut = True
        c0 = emit_all(S0, WStream(3, 2), WStream(3, 0))
        ctx["ratio"] = c0["nm"] / (0.75 * c0["nf"])
        print("segments", c0["nm"], c0["nf"], ctx["ratio"])
        S1 = Sched(nc)
        S1.cut = True
        WF1, WM1 = WStream(3, 2), WStream(3, 0)
        emit_all(S1, WF1, WM1)
        S = Sched(nc)
        WF, WM = WStream(3, 2, WF1.plan), WStream(3, 0, WM1.plan)
        emit_all(S, WF, WM)
        assert WF.i == len(WF1.plan) and WM.i == len(WM1.plan)
        S.emit()
    return nc


def _host_consts():
    ident = np.eye(128, dtype=np.float32)
    s = np.arange(128)[:, None]
    t = np.arange(128)[None, :]
    trimask = ((s // 64 == t // 64) & (s <= t)).astype(np.float32)
    scanmask = np.ones((1, TB), np.float32)
    scanmask[0, ::64] = 0.0
    onesm = np.full((128, 128), 1.0 / 128.0, np.float32)
    return ident, trimask, scanmask, onesm


def _pack_vec(v):
    v = np.asarray(v, np.float32).reshape(-1)
    return v.reshape(-1, 128).T


_CACHE = {}


def kernel(x, mix_norm_w, w_in, hg_lb, hg_norm_w, conv_w, conv_b, lru_wa, lru_ba,
           lru_wx, lru_bx, lru_a, w_out, ffn_norm_w, w_gate_up, w_down, final_norm_w):
    x = np.asarray(x, np.float32)
    B = x.shape[0]
    ident, trimask, scanmask, onesm = _host_consts()
    pvec = np.zeros((128, NCOL), np.float32)
    pvec[:, C_NW1:C_NW1 + 8] = _pack_vec(mix_norm_w[0])
    pvec[:, C_NW2:C_NW2 + 8] = _pack_vec(ffn_norm_w[0])
    pvec[:, C_A0:C_A0 + 4] = _pack_vec(hg_lb[0])
    pvec[:, C_A1:C_A1 + 4] = _pack_vec(hg_lb[1])
    pvec[:, C_HGNW:C_HGNW + 1] = _pack_vec(hg_norm_w[0])
    cw = np.asarray(conv_w[0], np.float32)
    for t in range(4):
        for tap in range(4):
            pvec[:, C_CW + t * 4 + tap] = cw[tap, t * 128:(t + 1) * 128]
    pvec[:, C_CB:C_CB + 4] = _pack_vec(conv_b[0])
    pvec[:, C_BA:C_BA + 4] = _pack_vec(lru_ba[0])
    pvec[:, C_BX:C_BX + 4] = _pack_vec(lru_bx[0])
    pvec[:, C_LA:C_LA + 4] = _pack_vec(lru_a[0])
    wabd = np.zeros((128, 4, 128), np.float32)
    wxbd = np.zeros((128, 4, 128), np.float32)
    wa = np.asarray(lru_wa[0], np.float32)
    wx = np.asarray(lru_wx[0], np.float32)
    for t in range(4):
        for j in range(2):
            wabd[j * 64:(j + 1) * 64, t, j * 64:(j + 1) * 64] = wa[2 * t + j]
            wxbd[j * 64:(j + 1) * 64, t, j * 64:(j + 1) * 64] = wx[2 * t + j]
    shared = {
        "w_in": np.ascontiguousarray(w_in[0], np.float32),
        "w_out": np.ascontiguousarray(w_out[0], np.float32),
        "w_gu": np.ascontiguousarray(w_gate_up[0], np.float32),
        "w_down": np.ascontiguousarray(w_down[0], np.float32),
        "pvec": pvec, "wabd": wabd, "wxbd": wxbd,
        "finw": np.asarray(final_norm_w, np.float32).reshape(1, D),
        "ident": ident, "trimask": trimask, "scanmask": scanmask, "onesm": onesm,
    }
    if "nc" not in _CACHE:
        _CACHE["nc"] = build_program()
    nc = _CACHE["nc"]
    in_maps = []
    for c in range(B):
        m = dict(shared)
        m["x"] = np.ascontiguousarray(x[c])
        in_maps.append(m)
    res = run_bass_kernel_spmd(nc, in_maps, core_ids=list(range(B)))
    return np.stack([np.asarray(r["out"], np.float32) for r in res.results], axis=0)
```

```python
import math
import numpy as np
from contextlib import ExitStack
import concourse.bass as bass
import concourse.mybir as mybir
from concourse.bass_utils import run_bass_kernel_spmd

F32 = mybir.dt.float32
BF16 = mybir.dt.bfloat16
AF = mybir.ActivationFunctionType
ALU = mybir.AluOpType

ENGS = ("pe", "act", "dve", "pool", "sp")

D = 1024
SEQ = 4096
TB = 512
NB = SEQ // TB
NT = TB // 128
KC = D // 128
DIN = 3072
DFF = 2816
NF = DFF // 128
NG = NF // 2
EPS = 1e-6
NCOL = 57
C_NW1, C_NW2, C_A0, C_A1, C_HGNW, C_CW, C_CB, C_BA, C_BX, C_LA = 0, 8, 16, 20, 24, 25, 41, 45, 49, 53
V_A, V_B, V_NB, V_HBA, V_HBX, V_C, V_HC, V_EPS, V_ONE, V_LN05, V_TMP = 0, 4, 8, 12, 16, 20, 24, 28, 29, 30, 32
NDV = 48


class Sched:
    XLAT = 250.0
    TBL = 1400.0
    WIN = 150.0
    COAL = 0
    COAL_MARGIN = 300.0

    def __init__(self, nc):
        self.nc = nc
        self.ops = []
        self.lw = {}
        self.rd = {}
        self.cut = False

    def stage(self, k):
        pass

    @staticmethod
    def _is_ps(k):
        n = k[0] if isinstance(k, tuple) else k
        return n in ("pF", "pM", "pO", "pX")

    def _add(self, eng, fn, reads, writes, dur, dsem=None, n=0, aset=None, lat=0.0):
        if self.cut:
            return
        writes = tuple(writes) + tuple(k for k in reads if self._is_ps(k))
        reads = tuple(k for k in reads if not self._is_ps(k))
        idx = len(self.ops)
        preds = set()
        for r in reads:
            p = self.lw.get(r)
            if p is not None:
                preds.add(p)
        for w in writes:
            p = self.lw.get(w)
            if p is not None:
                preds.add(p)
            preds.update(self.rd.get(w, ()))
        preds.discard(idx)
        for r in reads:
            self.rd.setdefault(r, []).append(idx)
        for w in writes:
            self.lw[w] = idx
            self.rd[w] = []
        import sys as _s
        f = _s._getframe(2)
        lines = []
        while f is not None and len(lines) < 3:
            lines.append(f.f_lineno)
            f = f.f_back
        self.ops.append(dict(eng=eng, fn=fn, preds=sorted(preds), dur=float(dur), dsem=dsem, n=n, aset=aset, lat=float(lat), lines=lines))

    def op(self, eng, fn, reads=(), writes=(), dur=500.0, aset=None):
        self._add(eng, fn, reads, writes, dur, aset=aset)

    def dma(self, eng, fn, sem, n=1, reads=(), writes=(), nbytes=1 << 19):
        self._add(eng, fn, reads, writes, 60.0 * n, dsem=("dma", sem), n=n, lat=2000.0 + nbytes / 150.0)

    def final_wait(self, eng, keys):
        self._add(eng, None, (), tuple(keys), 10.0)

    def schedule(self):
        ops = self.ops
        n = len(ops)
        succs = [[] for _ in range(n)]
        indeg = [0] * n
        for i, o in enumerate(ops):
            indeg[i] = len(o["preds"])
            for p in o["preds"]:
                succs[p].append(i)
        tail = [0.0] * n
        for i in range(n - 1, -1, -1):
            t = 0.0
            for s_ in succs[i]:
                if tail[s_] > t:
                    t = tail[s_]
            tail[i] = t + ops[i]["dur"] + ops[i]["lat"]
        ready = {e: [] for e in ENGS}
        rtime = [0.0] * n
        fin = [0.0] * n
        for i in range(n):
            if indeg[i] == 0:
                ready[ops[i]["eng"]].append(i)
        free = {e: 0.0 for e in ENGS}
        cur_set = [None]
        order = {e: [] for e in ENGS}
        done = 0
        while done < n:
            best = None
            for e in ENGS:
                rl = ready[e]
                if not rl:
                    continue
                st_min = None
                for i in rl:
                    st = max(free[e], rtime[i])
                    if e == "act" and ops[i]["aset"] is not None and cur_set[0] is not None and ops[i]["aset"] != cur_set[0]:
                        st += self.TBL
                    if st_min is None or st < st_min:
                        st_min = st
                cand = None
                for i in rl:
                    st = max(free[e], rtime[i])
                    if e == "act" and ops[i]["aset"] is not None and cur_set[0] is not None and ops[i]["aset"] != cur_set[0]:
                        st += self.TBL
                    if st <= st_min + self.WIN:
                        key = (-tail[i], i)
                        if cand is None or key < cand[0]:
                            cand = (key, i, st)
                if best is None or cand[2] < best[2]:
                    best = (e, cand[1], cand[2])
            e, i, st = best
            o = ops[i]
            ready[e].remove(i)
            if e == "act" and o["aset"] is not None:
                cur_set[0] = o["aset"]
            o["bind"] = ("eng", order[e][-1] if order[e] else None) if free[e] >= rtime[i] else ("dep", max(o["preds"], key=lambda p: fin[p]) if o["preds"] else None)
            o["start"] = st
            free[e] = st + o["dur"]
            fin[i] = st + o["dur"] + o["lat"]
            order[e].append(i)
            done += 1
            for s_ in succs[i]:
                t = fin[i] + (self.XLAT if ops[s_]["eng"] != e else 0.0)
                if t > rtime[s_]:
                    rtime[s_] = t
                indeg[s_] -= 1
                if indeg[s_] == 0:
                    ready[ops[s_]["eng"]].append(s_)
        self.order = order
        self.model_time = max(fin) if n else 0.0
        return order

    def emit(self):
        nc = self.nc
        ops = self.ops
        order = self.schedule()
        tok = [None] * len(ops)
        dcount = {}
        for e in ENGS:
            c = 0
            for i in order[e]:
                o = ops[i]
                if o["fn"] is None:
                    continue
                if o["dsem"] is None:
                    c += 1
                    tok[i] = (e, c)
                else:
                    dcount[o["dsem"]] = dcount.get(o["dsem"], 0) + o["n"]
                    tok[i] = (o["dsem"], 16 * dcount[o["dsem"]])
        with ExitStack() as st:
            sems = {}
            for e in ENGS:
                sems[e] = st.enter_context(nc.semaphore("s_" + e))
            for k, sk in enumerate(dcount):
                sems[sk] = st.enter_context(nc.semaphore("d%d" % k))
            block = st.enter_context(nc.Block())

            fin_of = {}
            for i, o in enumerate(ops):
                if tok[i] is not None:
                    fin_of[tok[i]] = o["start"] + o["dur"] + o["lat"]

            def needs_of(e_name, i):
                need = {}
                for p in ops[i]["preds"]:
                    t = tok[p]
                    if t is None:
                        continue
                    sk, v = t
                    if sk == e_name and e_name == "pe":
                        continue
                    if need.get(sk, 0) < v:
                        need[sk] = v
                return need

            def run(e_name, e):
                waited = {}
                seq = order[e_name]
                needs = [needs_of(e_name, i) for i in seq]
                for j, i in enumerate(seq):
                    o = ops[i]
                    for sk, v in needs[j].items():
                        if waited.get(sk, 0) >= v:
                            continue
                        for jj in range(j + 1, min(j + 1 + self.COAL, len(seq))):
                            v2 = needs[jj].get(sk, 0)
                            if v2 > v and fin_of.get((sk, v2), 1e30) + self.COAL_MARGIN <= o["start"]:
                                v = v2
                        waited[sk] = v
                        e.wait_ge(sems[sk], v)
                    if o["fn"] is None:
                        continue
                    r = o["fn"](e)
                    if o["dsem"] is None:
                        if isinstance(r, (list, tuple)):
                            r = r[-1]
                        r.then_inc(sems[e_name], 1)
                    else:
                        if not isinstance(r, (list, tuple)):
                            r = [r]
                        assert len(r) == o["n"], (len(r), o["n"])
                        for ins in r:
                            ins.then_inc(sems[o["dsem"]], 16)
                self.nwaits = getattr(self, "nwaits", {})
                self.nwaits[e_name] = sum(1 for _ in ())

            @block.tensor
            def _(e):
                run("pe", e)

            @block.scalar
            def _(e):
                run("act", e)

            @block.vector
            def _(e):
                run("dve", e)

            @block.gpsimd
            def _(e):
                run("pool", e)

            @block.sync
            def _(e):
                run("sp", e)


class WStream:
    def __init__(self, ns, la, plan=None):
        self.ns = ns
        self.la = la
        self.dry = plan is None
        self.plan = [] if plan is None else plan
        self.i = 0
        self.emitted = 0

    def get(self, spec, emit_load):
        i = self.i
        self.i += 1
        if self.dry:
            self.plan.append(spec)
            return i % self.ns
        assert self.plan[i] == spec, (i, self.plan[i], spec)
        hi = min(i + self.la, len(self.plan) - 1)
        while self.emitted <= hi:
            emit_load(self.plan[self.emitted], self.emitted % self.ns)
            self.emitted += 1
        return i % self.ns


def build_program(nblocks=NB):
    nc = bass.Bass("TRN2", target_bir_lowering=False)

    def din(name, shape, dt=F32):
        return nc.dram_tensor(name, shape, dt, kind="ExternalInput").ap()

    x_d = din("x", [SEQ, D])
    win_d = din("w_in", [D, DIN])
    wout_d = din("w_out", [D, D])
    wgu_d = din("w_gu", [D, 2 * DFF])
    wdn_d = din("w_down", [DFF, D])
    pvec_d = din("pvec", [128, NCOL])
    wabd_d = din("wabd", [128, 4, 128])
    wxbd_d = din("wxbd", [128, 4, 128])
    finw_d = din("finw", [1, D])
    ident_d = din("ident", [128, 128])
    trim_d = din("trimask", [128, 128])
    scanm_d = din("scanmask", [1, TB])
    ones_d = din("onesm", [128, 128])
    out_d = nc.dram_tensor("out", [SEQ, D], F32, kind="ExternalOutput").ap()
    scr = {
        "s_in": nc.dram_tensor("s_in", [6, 128, 4096], BF16, kind="Internal").ap(),
        "s_gu": nc.dram_tensor("s_gu", [NG, 128, 4096], BF16, kind="Internal").ap(),
        "s_dn": nc.dram_tensor("s_dn", [6, 128, 4096], BF16, kind="Internal").ap(),
    }

    with ExitStack() as st:
        def sb(name, shape, dt=F32):
            return st.enter_context(nc.sbuf_tensor(name, shape, dt))

        def ps(name, shape, dt=F32):
            return st.enter_context(nc.psum_tensor(name, shape, dt))

        wrF = sb("wrF", [128, 3, 4096], BF16)
        wrM = sb("wrM", [128, 3, 4096], BF16)
        wout = sb("wout", [128, KC, D], BF16)
        hb = sb("hb", [128, 8, D], F32)
        actA = sb("actA", [128, 2, KC, TB], BF16)
        actB = sb("actB", [128, KC, TB], BF16)
        U = sb("U", [128, NF * TB // 2], F32)
        Ub = U.bitcast(BF16)
        M = sb("M", [128, 14 * TB], F32)
        Mb = M.bitcast(BF16)
        sgt = sb("sgt", [128, 2, TB], F32)
        kdlo = sb("kdlo", [128, 2, NT, 128], BF16)
        kdhi = sb("kdhi", [128, 2, NT, 128], BF16)
        QD = sb("QD", [128, 2, TB], BF16)
        KD = sb("KD", [128, 2, TB], BF16)
        scm = sb("scm", [128, 2, NT, 128], BF16)
        sgs = sb("sgs", [128, 4, TB], BF16)
        Sring = sb("Sring", [128, 8, 128], F32)
        Sm = sb("Sm", [128, 4, 128], F32)
        dcol = sb("dcol", [128, 2, 8], F32)
        xbuf = sb("xbuf", [128, 1, TB + 3], F32)
        halo = sb("halo", [128, 4, 3], F32)
        hcar = sb("hcar", [128, 4], F32)
        xcb = sb("xcb", [128, 1, TB], BF16)
        identb = sb("identb", [128, 128], BF16)
        trimb = sb("trimb", [128, 128], BF16)
        scanmb = sb("scanmb", [128, TB], BF16)
        onesf = sb("onesf", [128, 128], F32)
        finw = sb("finw_s", [128, D], F32)
        wab = sb("wab", [128, 4, 128], BF16)
        wxb = sb("wxb", [128, 4, 128], BF16)
        pv = sb("pv", [128, NCOL], F32)
        dv = sb("dv", [128, NDV], F32)
        ssq = sb("ssq", [128, 16], F32)
        rstd = sb("rstd", [128, 16], F32)

        pF = [ps("pf%d" % i, [128, TB], F32) for i in range(4)]
        pM = [ps("pm%d" % i, [128, TB], F32) for i in range(2)]
        pO = ps("po", [128, TB], F32)
        pX = ps("px", [128, TB], F32)
        pXb = pX.bitcast(BF16)

        ctx = {}

        def emit_all(S, WF, WM):
            st8 = {"m": 0, "f": 0, "set": "S", "nm": 0, "nf": 0}

            def T(i):
                return M[:, i * TB:(i + 1) * TB]

            def TK(i):
                return (("M", i),)

            def ACT(out, in_, func, reads, writes, scale=1.0, bias=None, accum=None):
                if func in (AF.Silu, AF.Tanh):
                    st8["set"] = "S"
                elif func in (AF.Ln, AF.Exp):
                    st8["set"] = "L"

                def fn(e):
                    kw = dict(out=out, in_=in_, func=func, scale=scale)
                    if bias is not None:
                        kw["bias"] = bias
                    if accum is not None:
                        kw["accum_out"] = accum
                    return e.activation(**kw)
                aset = "S" if func in (AF.Silu, AF.Tanh) else ("L" if func in (AF.Ln, AF.Exp) else None)
                S.op("act", fn, reads, writes, dur=220.0 + 0.833 * in_.free_size() + (100.0 if accum is not None else 0.0), aset=aset)

            def TS(out, in0, s1, s2, op0, op1, reads, writes, eng="dve"):
                def fn(e):
                    if s2 is None:
                        return e.tensor_scalar(out=out, in0=in0, scalar1=s1, scalar2=None, op0=op0)
                    return e.tensor_scalar(out=out, in0=in0, scalar1=s1, scalar2=s2, op0=op0, op1=op1)
                S.op(eng, fn, reads, writes, dur=150.0 + 0.7 * out.free_size())

            def TT(out, in0, in1, op, reads, writes, eng="dve"):
                S.op(eng, lambda e: e.tensor_tensor(out=out, in0=in0, in1=in1, op=op), reads, writes, dur=150.0 + 1.04 * out.free_size())

            def STT(out, in0, scalar, in1, op0, op1, reads, writes):
                S.op("dve", lambda e: e.scalar_tensor_tensor(out=out, in0=in0, scalar=scalar, in1=in1, op0=op0, op1=op1), reads, writes, dur=150.0 + 1.04 * out.free_size())

            def SCAN(out, d0, d1, init, reads, writes):
                S.op("dve", lambda e: e.tensor_tensor_scan(out=out, data0=d0, data1=d1, initial=init, op0=ALU.mult, op1=ALU.add), reads, writes, dur=150.0 + 2.1 * out.free_size())

            def COPY(eng, out, in_, reads, writes):
                if eng == "act":
                    ACT(out, in_, AF.Copy, reads, writes)
                else:
                    S.op(eng, lambda e: e.tensor_copy(out=out, in_=in_), reads, writes, dur=150.0 + 0.6 * out.free_size())

            def MEMSET(eng, ap, val, writes):
                S.op(eng, lambda e: e.memset(ap, val), (), writes, dur=200.0)

            def mbank():
                i = st8["m"] % 2
                st8["m"] += 1
                return i

            def col(c):
                return pv[:, c:c + 1]

            def dcolv(c):
                return dv[:, c:c + 1]

            def mk_load(ring, rk):
                def emit_load(spec, slot):
                    name, idx = spec
                    S.dma("sp", lambda e: e.dma_start(out=ring[:, slot, :], in_=scr[name][idx]), (rk, slot),
                          reads=[(name, idx)], writes=[(rk, slot)], nbytes=1 << 20)
                return emit_load
            loadF = mk_load(wrF, "wrF")
            loadM = mk_load(wrM, "wrM")

            def wget(name, idx):
                if name == "s_in":
                    slot = WM.get((name, idx), loadM)
                    return ("wrM", slot), wrM[:, slot, :].rearrange("p (k c) -> p k c", k=8)
                slot = WF.get((name, idx), loadF)
                return ("wrF", slot), wrF[:, slot, :].rearrange("p (k c) -> p k c", k=8)

            def load_x(b):
                for tt in range(NT):
                    s = (b % 2) * 4 + tt
                    r0 = b * TB + tt * 128
                    S.dma("pool", (lambda s, r0: lambda e: e.dma_start(out=hb[:, s, :], in_=x_d[r0:r0 + 128, :]))(s, r0),
                          ("hb", s), writes=[("hb", s)])

            S.dma("sp", lambda e: e.dma_start(out=pv[:], in_=pvec_d), "pv", writes=["pv"])
            S.dma("sp", lambda e: e.dma_start(out=onesf[:], in_=ones_d), "onesf", writes=["onesf"])
            S.dma("sp", lambda e: e.dma_start(out=finw[:], in_=finw_d.partition_broadcast(128)), "finw", writes=["finw"])
            load_x(0)
            S.dma("pool", lambda e: e.dma_start(out=identb[:], in_=ident_d), "identb", writes=["identb"])
            S.dma("pool", lambda e: e.dma_start(out=trimb[:], in_=trim_d), "trimb", writes=["trimb"])
            S.dma("pool", lambda e: e.dma_start(out=scanmb[:], in_=scanm_d.partition_broadcast(128)), "scanmb", writes=["scanmb"])
            S.dma("pool", lambda e: e.dma_start(out=wab[:], in_=wabd_d), "wab", writes=["wab"])
            S.dma("pool", lambda e: e.dma_start(out=wxb[:], in_=wxbd_d), "wxb", writes=["wxb"])

            def img(name, idx):
                return scr[name][idx].rearrange("p (k c) -> p k c", k=8)

            cast_hist = []

            def cast_dep():
                return [cast_hist[-3]] if len(cast_hist) >= 3 else []

            def cast_in(idx, pieces):
                def fn(e):
                    r = []
                    for j, (c0, w) in enumerate(pieces):
                        dst = img("s_in", idx)[:, :, j * w:(j + 1) * w]
                        r.append(e.dma_start(out=dst, in_=win_d[:, c0:c0 + w].rearrange("(k p) c -> p k c", p=128)))
                    return r
                S.dma("pool", fn, ("s_in", idx), n=len(pieces), reads=cast_dep(), writes=[("s_in", idx)], nbytes=3 << 20)
                cast_hist.append(("s_in", idx))

            cast_in(0, [(1024, 512)])
            cast_in(3, [(1536, 512)])
            cast_in(1, [(0, 128), (512, 128), (128, 128), (640, 128)])
            cast_in(2, [(256, 128), (768, 128), (384, 128), (896, 128)])
            cast_in(4, [(2048, 512)])
            cast_in(5, [(2560, 512)])
            S.dma("pool", lambda e: e.dma_start(out=wout[:], in_=wout_d.rearrange("(e p) d -> p e d", p=128)), "wout",
                  reads=cast_dep(), writes=["wout"], nbytes=6 << 20)
            cast_hist.append("wout")

            def cast_gu(g):
                def fn(e):
                    a = e.dma_start(out=img("s_gu", g)[:, :, 0:256], in_=wgu_d[:, g * 256:(g + 1) * 256].rearrange("(k p) c -> p k c", p=128))
                    bb = e.dma_start(out=img("s_gu", g)[:, :, 256:512], in_=wgu_d[:, DFF + g * 256:DFF + (g + 1) * 256].rearrange("(k p) c -> p k c", p=128))
                    return [a, bb]
                S.dma("pool", fn, ("s_gu", g), n=2, reads=cast_dep(), writes=[("s_gu", g)], nbytes=3 << 20)
                cast_hist.append(("s_gu", g))

            FG = [(0, 8), (8, 8), (14, 8)]

            def cast_dn(j):
                dh, fg = j // 3, j % 3
                f0, nf = FG[fg]
                S.dma("pool", lambda e: e.dma_start(out=img("s_dn", j)[:, 0:nf, :],
                                                    in_=wdn_d[f0 * 128:(f0 + nf) * 128, dh * 512:(dh + 1) * 512].rearrange("(f p) c -> p f c", p=128)),
                      ("s_dn", j), reads=cast_dep(), writes=[("s_dn", j)], nbytes=3 << 20)
                cast_hist.append(("s_dn", j))

            for g in range(NG):
                cast_gu(g)
            for j in range(6):
                cast_dn(j)

            MEMSET("dve", Sm[:], 0.0, [("Sm", h) for h in range(4)])
            MEMSET("dve", halo[:], 0.0, ["halo"])
            MEMSET("dve", hcar[:], 0.0, ["hcar"])
            MEMSET("dve", kdlo[:], 0.0, [("kdlo", 0), ("kdlo", 1)])
            MEMSET("dve", kdhi[:], 0.0, [("kdhi", 0), ("kdhi", 1)])
            MEMSET("dve", dv[:, V_EPS:V_EPS + 1], EPS, ["dv_c"])
            MEMSET("dve", dv[:, V_ONE:V_ONE + 1], 1.0, ["dv_c"])
            MEMSET("dve", dv[:, V_LN05:V_LN05 + 1], math.log(0.5), ["dv_c"])
            TT(dv[:, V_TMP:V_TMP + 4], pv[:, C_A0:C_A0 + 4], pv[:, C_A1:C_A1 + 4], ALU.subtract, ["pv"], ["dv_t"])
            ACT(dv[:, V_TMP:V_TMP + 4], dv[:, V_TMP:V_TMP + 4], AF.Tanh, ["dv_t"], ["dv_t"], scale=0.5)
            TS(dv[:, V_A:V_A + 4], dv[:, V_TMP:V_TMP + 4], 0.25, 0.75, ALU.mult, ALU.add, ["dv_t"], ["dv_ab"])
            TS(dv[:, V_B:V_B + 4], dv[:, V_TMP:V_TMP + 4], -0.25, 0.25, ALU.mult, ALU.add, ["dv_t"], ["dv_ab"])
            TS(dv[:, V_NB:V_NB + 4], dv[:, V_TMP:V_TMP + 4], 0.25, -0.25, ALU.mult, ALU.add, ["dv_t"], ["dv_ab"])
            TS(dv[:, V_HBA:V_HBA + 4], pv[:, C_BA:C_BA + 4], 0.5, None, ALU.mult, None, ["pv"], ["dv_lru"])
            TS(dv[:, V_HBX:V_HBX + 4], pv[:, C_BX:C_BX + 4], 0.5, None, ALU.mult, None, ["pv"], ["dv_lru"])
            ACT(dv[:, V_TMP + 4:V_TMP + 8], pv[:, C_LA:C_LA + 4], AF.Exp, ["pv"], ["dv_t2"], scale=-1.0)
            ACT(dv[:, V_TMP + 4:V_TMP + 8], dv[:, V_TMP + 4:V_TMP + 8], AF.Ln, ["dv_t2", "dv_c"], ["dv_t2"], bias=dcolv(V_ONE))
            TS(dv[:, V_C:V_C + 4], dv[:, V_TMP + 4:V_TMP + 8], -8.0, None, ALU.mult, None, ["dv_t2"], ["dv_lru"])
            TS(dv[:, V_HC:V_HC + 4], dv[:, V_TMP + 4:V_TMP + 8], -4.0, None, ALU.mult, None, ["dv_t2"], ["dv_lru"])

            def rms_stats(slots, base, junk):
                for tt in range(NT):
                    s = slots[tt]
                    jk = junk[tt % 2]
                    ACT(Mb[:, jk * 1024:(jk + 1) * 1024], hb[:, s, :], AF.Square, [("hb", s)],
                        [("M", jk), ("ssq", base + tt)], accum=ssq[:, base + tt:base + tt + 1])
                ks = [("ssq", base + tt) for tt in range(NT)]
                kr = [("rstd", base + tt) for tt in range(NT)]
                ACT(rstd[:, base:base + 4], ssq[:, base:base + 4], AF.Ln, ks + ["dv_c"], kr, scale=1.0 / D, bias=dcolv(V_EPS))
                ACT(rstd[:, base:base + 4], rstd[:, base:base + 4], AF.Exp, kr, kr, scale=-0.5)

            def norm_transpose(slots, base, dst, dkey, nwc):
                for tt in range(NT):
                    s = slots[tt]
                    jk = tt % 2
                    xnb = Mb[:, jk * 1024:(jk + 1) * 1024]
                    kx = [("M", jk)]
                    TS(xnb, hb[:, s, :], rstd[:, base + tt:base + tt + 1], None, ALU.mult, None,
                       [("hb", s), ("rstd", base + tt)], kx)
                    yield

                    for hf, (pb_, pk_) in enumerate(((pX, "pX"), (pO, "pO"))):
                        def tr(e, xnb=xnb, hf=hf, pb_=pb_):
                            r = None
                            for q in range(4):
                                kc = hf * 4 + q
                                r = e.matmul(pb_[:, q * 128:(q + 1) * 128], lhsT=xnb[:, kc * 128:(kc + 1) * 128], rhs=identb[:],
                                             start=True, stop=True)
                            return r
                        S.op("pe", tr, kx + ["identb"], [pk_], dur=260.0)
                        TT(dst[:, hf * 4:hf * 4 + 4, tt * 128:(tt + 1) * 128], pb_[:, :].rearrange("p (k c) -> p k c", k=4),
                           pv[:, nwc + hf * 4:nwc + hf * 4 + 4].unsqueeze(2).to_broadcast([128, 4, 128]), ALU.mult,
                           [pk_, "pv"], [dkey])

            def proj_fm(wv, slot, j, bank, src, skey):
                def fn(e):
                    r = None
                    for kc in range(KC):
                        r = e.matmul(pM[bank][:, :], lhsT=wv[:, kc, j * 128:(j + 1) * 128], rhs=src[:, kc, :],
                                     start=(kc == 0), stop=(kc == KC - 1))
                    return r
                S.op("pe", fn, [slot, skey], [("pM", bank)], dur=1750.0)

            vtm = Mb[:, 24 * TB:28 * TB].rearrange("p (t c) -> p t c", t=NT)

            def vk(tt):
                return ("M", 12 + tt // 2)

            def hgrn2_head(h, tl, wslot, wv, jq, jf, A, akey):
                par = h % 2
                t1, t2, t3, t4 = tl
                bq = mbank()
                proj_fm(wv, wslot, jq, bq, A, akey)
                ACT(T(t1), pM[bq][:, :], AF.Silu, [("pM", bq)], TK(t1))
                bf_ = mbank()
                proj_fm(wv, wslot, jf, bf_, A, akey)
                ACT(T(t2), pM[bf_][:, :], AF.Tanh, [("pM", bf_)], TK(t2), scale=0.5)
                yield
                ACT(T(t3), T(t2), AF.Ln, TK(t2) + ("dv_ab",), TK(t3), scale=dcolv(V_B + h), bias=dcolv(V_A + h))
                TS(T(t2), T(t2), dcolv(V_NB + h), dcolv(V_B + h), ALU.mult, ALU.add, TK(t2) + ("dv_ab",), TK(t2))
                SCAN(T(t4), scanmb[:, :], T(t3), 0.0, TK(t3) + ("scanmb",), TK(t4))
                b3 = T(t4).rearrange("p (c j) -> p c j", j=64)
                ACT(dcol[:, par, :], T(t4)[:, 63:TB:64], AF.Exp, TK(t4), [("dcol", par)])
                TT(T(t3).rearrange("p (c j) -> p c j", j=64), b3[:, :, 63:64].to_broadcast([128, 8, 64]), b3,
                   ALU.subtract, TK(t4), TK(t3))
                yield
                ACT(T(t4), T(t3), AF.Exp, TK(t3), TK(t4), scale=-1.0)
                ACT(T(t3), T(t3), AF.Exp, TK(t3), TK(t3))
                TT(T(t1), T(t1), T(t4), ALU.mult, TK(t1) + TK(t4), TK(t1))
                COPY("act", QD[:, par, :], T(t1), TK(t1), [("QD", par)])
                TT(KD[:, par, :], T(t2), T(t3), ALU.mult, TK(t2) + TK(t3), [("KD", par)])
                q3 = T(t1).rearrange("p (c j) -> p c j", j=64)
                TT(q3, q3, dcol[:, par, :].unsqueeze(2).to_broadcast([128, 8, 64]), ALU.mult,
                   TK(t1) + (("dcol", par),), TK(t1))
                yield

                def trk(e):
                    r = None
                    for tt in range(NT):
                        r = e.matmul(pX[:, tt * 128:(tt + 1) * 128], lhsT=KD[:, par, tt * 128:(tt + 1) * 128], rhs=identb[:],
                                     start=True, stop=True)
                    return r
                S.op("pe", trk, [("KD", par), "identb"], ["pX"], dur=260.0)
                trv = pX[:, :].rearrange("p (t c) -> p t c", t=NT)
                COPY("act", kdlo[0:64, par, :, :], trv[0:64, :, :], ["pX"], [("kdlo", par)])
                COPY("act", kdhi[64:128, par, :, :], trv[64:128, :, :], ["pX"], [("kdhi", par)])
                yield
                pSC = pX[:, :].rearrange("p (t c) -> p t c", t=NT)

                def sc(e):
                    r = None
                    for tt in range(NT):
                        r = e.matmul(pSC[:, tt, :], lhsT=KD[:, par, tt * 128:(tt + 1) * 128],
                                     rhs=QD[:, par, tt * 128:(tt + 1) * 128], start=True, stop=True)
                    return r
                S.op("pe", sc, [("KD", par), ("QD", par)], ["pX"], dur=300.0)
                TT(scm[:, par, :, :], pSC, trimb[:, :].unsqueeze(1).to_broadcast([128, NT, 128]), ALU.mult,
                   ["pX", "trimb"], [("scm", par)])
                yield

            def hgrn2_chain(h, tl):
                par = h % 2
                t1, t2, t3, t4 = tl
                pSC = pX[:, :].rearrange("p (t c) -> p t c", t=NT)

                def state(c):
                    return (Sm[:, h, :], ("Sm", h)) if c == 0 else (Sring[:, c - 1, :], ("Sring", c - 1))

                for half4 in range(2):
                    cs = range(half4 * 4, half4 * 4 + 4)

                    def pmm(e, cs=cs):
                        r = None
                        for c in cs:
                            tt, half = c // 2, c % 2
                            kd = kdlo if half == 0 else kdhi
                            r = e.matmul(pSC[:, c % 4, :], lhsT=kd[:, par, tt, :], rhs=vtm[:, tt, h * 128:(h + 1) * 128],
                                         start=True, stop=True)
                        return r
                    S.op("pe", pmm, [("kdlo", par), ("kdhi", par), ("M", 12), ("M", 13)], ["pX"], dur=300.0)
                    for c in cs:
                        sin, sink = state(c)
                        sout, soutk = (Sm[:, h, :], ("Sm", h)) if c == 7 else (Sring[:, c, :], ("Sring", c))
                        STT(sout, sin, dcol[:, par, c:c + 1], pSC[:, c % 4, :], ALU.mult, ALU.add,
                            [sink, ("dcol", par), "pX"], [soutk])
                    yield

                    def om(e, cs=cs):
                        r = None
                        for c in cs:
                            tt, half = c // 2, c % 2
                            e.matmul(pO[:, c * 64:(c + 1) * 64], lhsT=vtm[:, tt, h * 128:(h + 1) * 128],
                                     rhs=scm[:, par, tt, half * 64:(half + 1) * 64], start=True, stop=False)
                            r = e.matmul(pO[:, c * 64:(c + 1) * 64], lhsT=state(c)[0],
                                         rhs=T(t1)[:, c * 64:(c + 1) * 64], start=False, stop=True)
                        return r
                    S.op("pe", om, [("M", 12), ("M", 13), ("scm", par)] + list(TK(t1)) + [state(c)[1] for c in cs], ["pO"], dur=1300.0)
                yield
                ACT(T(t1), pO[:, :], AF.Square, ["pO"], TK(t1))
                yield
                bs = mbank()
                S.op("pe", lambda e: e.matmul(pM[bs][:, :], lhsT=onesf[:, :], rhs=T(t1), start=True, stop=True),
                     ["onesf"] + list(TK(t1)), [("pM", bs)], dur=900.0)
                ACT(T(t2), pM[bs][:, :], AF.Ln, [("pM", bs), "dv_c"], TK(t2), bias=dcolv(V_EPS))
                ACT(T(t2), T(t2), AF.Exp, TK(t2), TK(t2), scale=-0.5)
                TT(T(t3), pO[:, :], T(t2), ALU.mult, ["pO"] + list(TK(t2)), TK(t3))
                STT(actB[:, h, :], T(t3), col(C_HGNW), sgs[:, h, :], ALU.mult, ALU.mult,
                    TK(t3) + ("pv", ("sgs", h)), [("actB", h)])
                yield

            def lru_tile(t, tl, xslot, xv, A, akey):
                par = 0
                txc, tr_, ti, ta = tl
                bx_ = mbank()
                proj_fm(xv, xslot, t, bx_, A, akey)
                COPY("dve", xbuf[:, par, 0:3], halo[:, t, :], ["halo"], [("xbuf", par)])
                COPY("act", xbuf[:, par, 3:TB + 3], pM[bx_][:, :], [("pM", bx_)], [("xbuf", par)])
                COPY("dve", halo[:, t, :], xbuf[:, par, TB:TB + 3], [("xbuf", par)], ["halo"])
                TS(T(txc), xbuf[:, par, 3:TB + 3], col(C_CW + t * 4 + 3), col(C_CB + t), ALU.mult, ALU.add,
                   [("xbuf", par), "pv"], TK(txc))
                for tap in range(3):
                    STT(T(txc), xbuf[:, par, tap:tap + TB], col(C_CW + t * 4 + tap), T(txc), ALU.mult, ALU.add,
                        [("xbuf", par), "pv"] + list(TK(txc)), TK(txc))
                COPY("act", xcb[:, par, :], T(txc), TK(txc), [("xcb", par)])
                yield
                br = mbank()
                S.op("pe", lambda e: e.matmul(pM[br][:, :], lhsT=wab[:, t, :], rhs=xcb[:, par, :], start=True, stop=True),
                     ["wab", ("xcb", par)], [("pM", br)], dur=260.0)
                ACT(T(tr_), pM[br][:, :], AF.Tanh, [("pM", br), "dv_lru"], TK(tr_), scale=0.5, bias=dcolv(V_HBA + t))
                bi = mbank()
                S.op("pe", lambda e: e.matmul(pM[bi][:, :], lhsT=wxb[:, t, :], rhs=xcb[:, par, :], start=True, stop=True),
                     ["wxb", ("xcb", par)], [("pM", bi)], dur=260.0)
                ACT(T(ti), pM[bi][:, :], AF.Tanh, [("pM", bi), "dv_lru"], TK(ti), scale=0.5, bias=dcolv(V_HBX + t))
                STT(T(ti), T(ti), 1.0, T(txc), ALU.add, ALU.mult, TK(ti) + TK(txc), TK(ti))
                yield
                ACT(T(ta), T(tr_), AF.Exp, TK(tr_) + ("dv_lru",), TK(ta), scale=dcolv(V_HC + t), bias=dcolv(V_HC + t))
                ACT(T(tr_), T(tr_), AF.Exp, TK(tr_) + ("dv_lru",), TK(tr_), scale=dcolv(V_C + t), bias=dcolv(V_C + t))
                ACT(T(tr_), T(tr_), AF.Ln, TK(tr_) + ("dv_c",), TK(tr_), scale=-1.0, bias=dcolv(V_ONE))
                ACT(T(tr_), T(tr_), AF.Exp, TK(tr_) + ("dv_c",), TK(tr_), scale=0.5, bias=dcolv(V_LN05))
                TT(T(ti), T(ti), T(tr_), ALU.mult, TK(ti) + TK(tr_), TK(ti))
                SCAN(T(tr_), T(ta), T(ti), hcar[:, t:t + 1], TK(ta) + TK(ti) + ("hcar",), TK(tr_))
                COPY("dve", hcar[:, t:t + 1], T(tr_)[:, TB - 1:TB], TK(tr_), ["hcar"])
                yield

            def lru_gate(t, tl, gslot, gv, A, akey):
                txc, tr_, ti, ta = tl
                bgt = mbank()
                proj_fm(gv, gslot, t, bgt, A, akey)
                ACT(T(txc), pM[bgt][:, :], AF.Square, [("pM", bgt)], TK(txc))
                TS(T(txc), T(txc), 0.044715, 1.0, ALU.mult, ALU.add, TK(txc), TK(txc))
                TT(T(txc), pM[bgt][:, :], T(txc), ALU.mult, [("pM", bgt)] + list(TK(txc)), TK(txc))
                ACT(T(txc), T(txc), AF.Tanh, TK(txc), TK(txc), scale=0.7978845608028654)
                STT(T(txc), T(txc), 1.0, pM[bgt][:, :], ALU.add, ALU.mult, TK(txc) + (("pM", bgt),), TK(txc))
                STT(actB[:, 4 + t, :], T(tr_), 0.5, T(txc), ALU.mult, ALU.mult, TK(tr_) + TK(txc), [("actB", 4 + t)])
                yield

            def zipgen(*gens):
                gens = list(gens)
                while gens:
                    for g in list(gens):
                        try:
                            next(g)
                            yield
                        except StopIteration:
                            gens.remove(g)

            def mixer(b):
                pb = b % 2
                A = actA[:, pb, :, :]
                akey = ("actA", pb)
                slots = [pb * 4 + tt for tt in range(NT)]
                if b > 0:
                    load_x(b)
                rms_stats(slots, 0, (0, 1))
                yield
                yield from norm_transpose(slots, 0, A, akey, C_NW1)
                yield
                vs, vv = wget("s_in", 0)
                for tt in range(NT):
                    bank = mbank()

                    def fn(e, bank=bank, tt=tt):
                        r = None
                        for kc in range(KC):
                            r = e.matmul(pM[bank][:, :], lhsT=A[:, kc, tt * 128:(tt + 1) * 128], rhs=vv[:, kc, :],
                                         start=(kc == 0), stop=(kc == KC - 1))
                        return r
                    S.op("pe", fn, [vs, akey], [("pM", bank)], dur=1750.0)
                    COPY("act", vtm[:, tt, :], pM[bank][:, :], [("pM", bank)], [vk(tt)])
                    if tt % 2 == 1:
                        yield
                gs, gv = wget("s_in", 3)
                for h in range(4):
                    bg = mbank()
                    proj_fm(gv, gs, h, bg, A, akey)
                    ACT(sgs[:, h, :], pM[bg][:, :], AF.Silu, [("pM", bg)], [("sgs", h)])
                    if h % 2 == 1:
                        yield
                def hg_all():
                    for hp in range(2):
                        ws, wv = wget("s_in", 1 + hp)
                        h0, h1 = 2 * hp, 2 * hp + 1
                        yield from zipgen(hgrn2_head(h0, (0, 1, 2, 3), ws, wv, 0, 1, A, akey),
                                          hgrn2_head(h1, (4, 5, 6, 7), ws, wv, 2, 3, A, akey))
                        yield from hgrn2_chain(h0, (0, 1, 2, 3))
                        yield from hgrn2_chain(h1, (4, 5, 6, 7))

                def lru_all():
                    xs, xv = wget("s_in", 4)
                    gts, gtv = wget("s_in", 5)
                    for t in range(4):
                        yield from lru_tile(t, (8, 9, 10, 11), xs, xv, A, akey)
                        yield from lru_gate(t, (8, 9, 10, 11), gts, gtv, A, akey)
                yield from zipgen(hg_all(), lru_all())
                akeys = [("actB", i) for i in range(8)]
                for tt in range(NT):
                    s = slots[tt]
                    for dh in range(2):
                        bank = mbank()

                        def fn(e, bank=bank, tt=tt, dh=dh):
                            r = None
                            for ec in range(KC):
                                r = e.matmul(pM[bank][:, :], lhsT=actB[:, ec, tt * 128:(tt + 1) * 128],
                                             rhs=wout[:, ec, dh * 512:(dh + 1) * 512], start=(ec == 0), stop=(ec == KC - 1))
                            return r
                        S.op("pe", fn, akeys + ["wout"], [("pM", bank)], dur=1750.0)
                        TT(hb[:, s, dh * 512:(dh + 1) * 512], pM[bank][:, :], hb[:, s, dh * 512:(dh + 1) * 512], ALU.add,
                           [("pM", bank), ("hb", s)], [("hb", s)])
                    yield
                rms_stats(slots, 4, (0, 1))
                yield
                yield from norm_transpose(slots, 4, A, akey, C_NW2)

            def ffn(b):
                pb = b % 2
                A = actA[:, pb, :, :]
                akey = ("actA", pb)
                slots = [pb * 4 + tt for tt in range(NT)]
                for g in range(NG):
                    sl, rs = wget("s_gu", g)
                    for j in range(2):
                        f = 2 * g + j
                        bgate = (st8["f"] % 2) * 2
                        bup = bgate + 1
                        st8["f"] += 1

                        def mmg(e, rs=rs, j=j, bgate=bgate):
                            r = None
                            for kc in range(KC):
                                r = e.matmul(pF[bgate][:, :], lhsT=rs[:, kc, j * 128:(j + 1) * 128], rhs=A[:, kc, :],
                                             start=(kc == 0), stop=(kc == KC - 1))
                            return r

                        def mmu(e, rs=rs, j=j, bup=bup):
                            r = None
                            for kc in range(KC):
                                r = e.matmul(pF[bup][:, :], lhsT=rs[:, kc, 256 + j * 128:256 + (j + 1) * 128], rhs=A[:, kc, :],
                                             start=(kc == 0), stop=(kc == KC - 1))
                            return r
                        S.op("pe", mmg, [sl, akey], [("pF", bgate)], dur=1725.0)
                        S.op("pe", mmu, [sl, akey], [("pF", bup)], dur=1725.0)
                        sp_ = f % 2
                        ACT(sgt[:, sp_, :], pF[bgate][:, :], AF.Silu, [("pF", bgate)], [("sgt", sp_)])
                        TT(Ub[:, f * TB:(f + 1) * TB], pF[bup][:, :], sgt[:, sp_, :], ALU.mult,
                           [("pF", bup), ("sgt", sp_)], [("U", f)])
                        yield
                for dh in range(2):
                    for fg in range(3):
                        f0, nf = FG[fg]
                        sl, rs = wget("s_dn", dh * 3 + fg)

                        fis = range(2, 8) if fg == 2 else range(8)
                        for fi in fis:
                            f = f0 + fi

                            def dmm(e, rs=rs, f=f, fi=fi):
                                r = None
                                for tt in range(NT):
                                    r = e.matmul(pF[tt][:, :], lhsT=Ub[:, f * TB + tt * 128:f * TB + (tt + 1) * 128],
                                                 rhs=rs[:, fi, :], start=(f == 0), stop=(f == NF - 1), skip_group_check=True)
                                return r
                            S.op("pe", dmm, [sl, ("U", f)], [("pF", tt) for tt in range(NT)], dur=215.0 * 4)
                        yield
                    for tt in range(NT):
                        s = slots[tt]
                        TT(hb[:, s, dh * 512:(dh + 1) * 512], pF[tt][:, :], hb[:, s, dh * 512:(dh + 1) * 512], ALU.add,
                           [("pF", tt), ("hb", s)], [("hb", s)])
                    yield
                for tt in range(NT):
                    s = slots[tt]
                    ACT(sgt[:, 0, :].bitcast(BF16), hb[:, s, :], AF.Square, [("hb", s)], [("sgt", 0), ("ssq", 8 + tt)],
                        accum=ssq[:, 8 + tt:9 + tt])
                ks = [("ssq", 8 + tt) for tt in range(NT)]
                kr = [("rstd", 8 + tt) for tt in range(NT)]
                ACT(rstd[:, 8:12], ssq[:, 8:12], AF.Ln, ks + ["dv_c"], kr, scale=1.0 / D, bias=dcolv(V_EPS))
                ACT(rstd[:, 8:12], rstd[:, 8:12], AF.Exp, kr, kr, scale=-0.5)
                yield
                for tt in range(NT):
                    s = slots[tt]
                    r0 = b * TB + tt * 128
                    STT(hb[:, s, :], hb[:, s, :], rstd[:, 8 + tt:9 + tt], finw[:, :], ALU.mult, ALU.mult,
                        [("hb", s), ("rstd", 8 + tt), "finw"], [("hb", s)])
                    S.dma("pool", (lambda s, r0: lambda e: e.dma_start(out=out_d[r0:r0 + 128, :], in_=hb[:, s, :]))(s, r0),
                          ("ost", s), reads=[("hb", s)], writes=[("hb", s), ("OUT", s)])
                    yield

            for _ in mixer(0):
                st8["nm"] += 1
            for b in range(nblocks):
                gm = mixer(b + 1) if b + 1 < nblocks else None
                acc = 0.0
                for _ in ffn(b):
                    if b == 0:
                        st8["nf"] += 1
                    if gm is not None:
                        acc += ctx["ratio"]
                        while acc >= 1.0 and gm is not None:
                            acc -= 1.0
                            try:
                                next(gm)
                            except StopIteration:
                                gm = None
                if gm is not None:
                    for _ in gm:
                        pass
            S.final_wait("sp", [("OUT", s) for s in range(8)])
            return st8

        ctx["ratio"] = 1.0
        S0 = Sched(nc)
        S0.cut = True
        c0 = emit_all(S0, WStream(3, 2), WStream(3, 0))
        ctx["ratio"] = c0["nm"] / (0.75 * c0["nf"])
        print("segments", c0["nm"], c0["nf"], ctx["ratio"])
        S1 = Sched(nc)
        S1.cut = True
        WF1, WM1 = WStream(3, 2), WStream(3, 0)
        emit_all(S1, WF1, WM1)
        S = Sched(nc)
        WF, WM = WStream(3, 2, WF1.plan), WStream(3, 0, WM1.plan)
        emit_all(S, WF, WM)
        assert WF.i == len(WF1.plan) and WM.i == len(WM1.plan)
        S.emit()
    return nc


def _host_consts():
    ident = np.eye(128, dtype=np.float32)
    s = np.arange(128)[:, None]
    t = np.arange(128)[None, :]
    trimask = ((s // 64 == t // 64) & (s <= t)).astype(np.float32)
    scanmask = np.ones((1, TB), np.float32)
    scanmask[0, ::64] = 0.0
    onesm = np.full((128, 128), 1.0 / 128.0, np.float32)
    return ident, trimask, scanmask, onesm


def _pack_vec(v):
    v = np.asarray(v, np.float32).reshape(-1)
    return v.reshape(-1, 128).T


_CACHE = {}


def kernel(x, mix_norm_w, w_in, hg_lb, hg_norm_w, conv_w, conv_b, lru_wa, lru_ba,
           lru_wx, lru_bx, lru_a, w_out, ffn_norm_w, w_gate_up, w_down, final_norm_w):
    x = np.asarray(x, np.float32)
    B = x.shape[0]
    ident, trimask, scanmask, onesm = _host_consts()
    pvec = np.zeros((128, NCOL), np.float32)
    pvec[:, C_NW1:C_NW1 + 8] = _pack_vec(mix_norm_w[0])
    pvec[:, C_NW2:C_NW2 + 8] = _pack_vec(ffn_norm_w[0])
    pvec[:, C_A0:C_A0 + 4] = _pack_vec(hg_lb[0])
    pvec[:, C_A1:C_A1 + 4] = _pack_vec(hg_lb[1])
    pvec[:, C_HGNW:C_HGNW + 1] = _pack_vec(hg_norm_w[0])
    cw = np.asarray(conv_w[0], np.float32)
    for t in range(4):
        for tap in range(4):
            pvec[:, C_CW + t * 4 + tap] = cw[tap, t * 128:(t + 1) * 128]
    pvec[:, C_CB:C_CB + 4] = _pack_vec(conv_b[0])
    pvec[:, C_BA:C_BA + 4] = _pack_vec(lru_ba[0])
    pvec[:, C_BX:C_BX + 4] = _pack_vec(lru_bx[0])
    pvec[:, C_LA:C_LA + 4] = _pack_vec(lru_a[0])
    wabd = np.zeros((128, 4, 128), np.float32)
    wxbd = np.zeros((128, 4, 128), np.float32)
    wa = np.asarray(lru_wa[0], np.float32)
    wx = np.asarray(lru_wx[0], np.float32)
    for t in range(4):
        for j in range(2):
            wabd[j * 64:(j + 1) * 64, t, j * 64:(j + 1) * 64] = wa[2 * t + j]
            wxbd[j * 64:(j + 1) * 64, t, j * 64:(j + 1) * 64] = wx[2 * t + j]
    shared = {
        "w_in": np.ascontiguousarray(w_in[0], np.float32),
        "w_out": np.ascontiguousarray(w_out[0], np.float32),
        "w_gu": np.ascontiguousarray(w_gate_up[0], np.float32),
        "w_down": np.ascontiguousarray(w_down[0], np.float32),
        "pvec": pvec, "wabd": wabd, "wxbd": wxbd,
        "finw": np.asarray(final_norm_w, np.float32).reshape(1, D),
        "ident": ident, "trimask": trimask, "scanmask": scanmask, "onesm": onesm,
    }
    if "nc" not in _CACHE:
        _CACHE["nc"] = build_program()
    nc = _CACHE["nc"]
    in_maps = []
    for c in range(B):
        m = dict(shared)
        m["x"] = np.ascontiguousarray(x[c])
        in_maps.append(m)
    res = run_bass_kernel_spmd(nc, in_maps, core_ids=list(range(B)))
    return np.stack([np.asarray(r["out"], np.float32) for r in res.results], axis=0)
```
